# Optimizing a Trainium2 kernel written in Bass

```python
import math
import jax, jax.numpy as jnp
from jax import lax
import numpy as np

D_MODEL = 2048
BATCH = 4
SEQ = 4096
DEPTH = 1

CHUNK = 64
D_FF = 5632
P_DIM = 256
SGU_BLOCK = 128
SGU_GROUPS = 8
SGU_GROUP_DIM = 128
SGU_WIDTH = SGU_GROUPS * SGU_GROUP_DIM
N_HEADS = 8
HEAD_DIM = 64
V_HEAD_DIM = 2 * HEAD_DIM
QK_WIDTH = N_HEADS * 2 * HEAD_DIM
ATTN_WIDTH = N_HEADS * V_HEAD_DIM
Q_BLOCK = 128
IN_COLS = 2 * SGU_WIDTH + 2 * QK_WIDTH + ATTN_WIDTH + 2 * D_MODEL
ALPHA = (2 * DEPTH) ** 0.25
BETA = (8 * DEPTH) ** -0.25
LN_EPS = 1e-5

kernel_name = "hybrid_gmlp_diffattn_macaron_deepnorm"


def layer_norm(x, g, b):
    xf = x.astype(jnp.float32)
    mu = jnp.mean(xf, axis=-1, keepdims=True)
    var = jnp.mean(jnp.square(xf - mu), axis=-1, keepdims=True)
    y = (xf - mu) * lax.rsqrt(var + LN_EPS)
    return (y * g.astype(jnp.float32) + b.astype(jnp.float32)).astype(x.dtype)


def swiglu(x, w_gu, w_down):
    gate, up = jnp.split(x @ w_gu, 2, axis=-1)
    return (jax.nn.silu(gate) * up) @ w_down


def gmlp_sgu(u, v, ln_g, ln_b, w_s, b_s):
    B, S, _ = u.shape
    v = layer_norm(v, ln_g, ln_b)
    pos = jnp.arange(SGU_BLOCK)
    allowed = (pos[None, :] // CHUNK) <= (pos[:, None] // CHUNK)
    w = jnp.where(allowed[None], w_s, jnp.zeros_like(w_s))
    vb = v.reshape(B, S // SGU_BLOCK, SGU_BLOCK, SGU_GROUPS, SGU_GROUP_DIM)
    s = jnp.einsum('gts,bnsgc->bntgc', w, vb) + b_s.T[None, None, :, :, None]
    return u * s.reshape(B, S, SGU_WIDTH)


def diff_attention(q, k, v, lam, slopes):
    B, S, H = q.shape[0], q.shape[1], q.shape[2]
    nq = S // Q_BLOCK
    scale = HEAD_DIM ** -0.5
    qb = q.reshape(B, nq, Q_BLOCK, H, 2, HEAD_DIM).transpose(1, 0, 2, 3, 4, 5)
    kpos = jnp.arange(S)

    def block(args):
        qi, qblk = args
        tpos = qi * Q_BLOCK + jnp.arange(Q_BLOCK)
        s = jnp.einsum('bqhmd,bkhmd->bhmqk', qblk, k).astype(jnp.float32) * scale
        dist = jnp.abs(tpos[:, None] - kpos[None, :]).astype(jnp.float32)
        allowed = (kpos[None, :] // CHUNK) <= (tpos[:, None] // CHUNK)
        s = s - slopes[None, :, None, None, None] * dist
        s = jnp.where(allowed, s, -jnp.inf)
        probs = jax.nn.softmax(s, axis=-1)
        a = probs[:, :, 0] - lam * probs[:, :, 1]
        return jnp.einsum('bhqk,bkhe->bqhe', a.astype(v.dtype), v)

    out = lax.map(block, (jnp.arange(nq), qb))
    return out.transpose(1, 0, 2, 3, 4).reshape(B, S, H, V_HEAD_DIM)


def head_rms_norm(o, g):
    of = o.astype(jnp.float32)
    y = of * lax.rsqrt(jnp.mean(jnp.square(of), axis=-1, keepdims=True) + LN_EPS)
    return (y * g.reshape(N_HEADS, V_HEAD_DIM).astype(jnp.float32)).astype(o.dtype)


def setup_inputs(seed: int = 0) -> dict:
    key = jax.random.key(seed)
    ks = iter(jax.random.split(key, 40))
    f32 = jnp.float32
    L = DEPTH

    def nrm(shape, scale):
        return jax.random.normal(next(ks), shape, f32) * scale

    def gain(shape):
        return 1.0 + nrm(shape, 0.05)

    def bias(shape):
        return nrm(shape, 0.02)

    return {
        "x": nrm((BATCH, SEQ, D_MODEL), 1.0),
        "p": nrm((DEPTH, BATCH, SEQ, P_DIM), 1.0),
        "ffn1_w_gu": nrm((L, D_MODEL, 2 * D_FF), D_MODEL ** -0.5),
        "ffn1_w_down": nrm((L, D_FF, D_MODEL), BETA * D_FF ** -0.5),
        "ln1_g": gain((L, D_MODEL)),
        "ln1_b": bias((L, D_MODEL)),
        "w_in": nrm((L, D_MODEL, IN_COLS), D_MODEL ** -0.5),
        "sgu_ln_g": gain((L, SGU_WIDTH)),
        "sgu_ln_b": bias((L, SGU_WIDTH)),
        "sgu_w": nrm((L, SGU_GROUPS, SGU_BLOCK, SGU_BLOCK), SGU_BLOCK ** -0.5),
        "sgu_b": 1.0 + nrm((L, SGU_GROUPS, SGU_BLOCK), 0.1),
        "lam_q1": nrm((L, HEAD_DIM), 0.1),
        "lam_k1": nrm((L, HEAD_DIM), 0.1),
        "lam_q2": nrm((L, HEAD_DIM), 0.1),
        "lam_k2": nrm((L, HEAD_DIM), 0.1),
        "attn_norm_g": gain((L, ATTN_WIDTH)),
        "w_branch_a": nrm((L, SGU_WIDTH, D_MODEL), SGU_WIDTH ** -0.5),
        "w_branch_b": nrm((L, ATTN_WIDTH, D_MODEL), ATTN_WIDTH ** -0.5),
        "w_out": nrm((L, D_MODEL, D_MODEL), BETA * D_MODEL ** -0.5),
        "ln2_g": gain((L, D_MODEL)),
        "ln2_b": bias((L, D_MODEL)),
        "ffn2_w_gu": nrm((L, D_MODEL, 2 * D_FF), D_MODEL ** -0.5),
        "ffn2_w_down": nrm((L, D_FF, D_MODEL), BETA * D_FF ** -0.5),
        "ln3_g": gain((L, D_MODEL)),
        "ln3_b": bias((L, D_MODEL)),
        "w_pe_gate": nrm((L, D_MODEL, D_MODEL), D_MODEL ** -0.5),
        "w_pe_proj": nrm((L, P_DIM, D_MODEL), BETA * P_DIM ** -0.5),
        "ln4_g": gain((L, D_MODEL)),
        "ln4_b": bias((L, D_MODEL)),
    }


def reference(x, p, ffn1_w_gu, ffn1_w_down, ln1_g, ln1_b, w_in, sgu_ln_g, sgu_ln_b,
              sgu_w, sgu_b, lam_q1, lam_k1, lam_q2, lam_k2, attn_norm_g, w_branch_a,
              w_branch_b, w_out, ln2_g, ln2_b, ffn2_w_gu, ffn2_w_down, ln3_g, ln3_b,
              w_pe_gate, w_pe_proj, ln4_g, ln4_b):
    B, S, _ = x.shape
    slopes = jnp.asarray(2.0 ** (-8.0 * np.arange(1, N_HEADS + 1) / N_HEADS), dtype=jnp.float32)
    splits = np.cumsum([SGU_WIDTH, SGU_WIDTH, QK_WIDTH, QK_WIDTH, ATTN_WIDTH, D_MODEL]).tolist()
    for i in range(DEPTH):
        lam_init = 0.8 - 0.6 * math.exp(-0.3 * i)
        x = layer_norm(ALPHA * x + 0.5 * swiglu(x, ffn1_w_gu[i], ffn1_w_down[i]), ln1_g[i], ln1_b[i])
        proj = x @ w_in[i]
        u, v, q, k, val, g_a, g_b = jnp.split(proj, splits, axis=-1)
        y_a = gmlp_sgu(jax.nn.gelu(u), jax.nn.gelu(v), sgu_ln_g[i], sgu_ln_b[i], sgu_w[i], sgu_b[i])
        lam = (jnp.exp(jnp.sum(lam_q1[i].astype(jnp.float32) * lam_k1[i].astype(jnp.float32)))
               - jnp.exp(jnp.sum(lam_q2[i].astype(jnp.float32) * lam_k2[i].astype(jnp.float32)))
               + lam_init)
        o = diff_attention(q.reshape(B, S, N_HEADS, 2, HEAD_DIM),
                           k.reshape(B, S, N_HEADS, 2, HEAD_DIM),
                           val.reshape(B, S, N_HEADS, V_HEAD_DIM), lam, slopes)
        y_b = (head_rms_norm(o, attn_norm_g[i]) * (1.0 - lam_init)).reshape(B, S, ATTN_WIDTH)
        merged = jax.nn.sigmoid(g_a) * (y_a @ w_branch_a[i]) + jax.nn.sigmoid(g_b) * (y_b @ w_branch_b[i])
        x = layer_norm(ALPHA * x + merged @ w_out[i], ln2_g[i], ln2_b[i])
        x = layer_norm(ALPHA * x + 0.5 * swiglu(x, ffn2_w_gu[i], ffn2_w_down[i]), ln3_g[i], ln3_b[i])
        e = jax.nn.sigmoid(x @ w_pe_gate[i]) * (p[i] @ w_pe_proj[i])
        x = layer_norm(ALPHA * x + e, ln4_g[i], ln4_b[i])
    return x
```

```python
import os
import math
from contextlib import ExitStack

import numpy as np
import ml_dtypes

import concourse.bass as bass
import concourse.mybir as mybir
from concourse.bass_utils import run_bass_kernel_spmd

F32 = mybir.dt.float32
BF16 = mybir.dt.bfloat16
AF = mybir.ActivationFunctionType
ALU = mybir.AluOpType

D = 2048
SEQ = 4096
NBATCH = 4
DFF = 5632
NJ = 44
NPART = 4
JP = 11
OWN = 2048
ALL = 4096
NQ = 4
QW = 512
ALPHA = float(2.0 ** 0.25)
LN_EPS = 1e-5
LAM_INIT = 0.8 - 0.6 * math.exp(0.0)
NEG = -30000.0
STAGE = int(os.environ.get("MK_STAGE", "99"))


class Buf:
    __slots__ = ("w", "r", "ds")

    def __init__(self):
        self.w = None
        self.r = {}
        self.ds = None


class DSem:
    __slots__ = ("h", "cnt", "key", "last")

    def __init__(self, h, key):
        self.h = h
        self.cnt = 0
        self.key = key
        self.last = None


class Prog:
    def __init__(self, nc):
        self.nc = nc
        self.eng = {"pe": nc.tensor, "act": nc.scalar, "dve": nc.vector, "pool": nc.gpsimd, "sp": nc.sync}
        self.sem = {k: nc.alloc_semaphore("s_" + k) for k in self.eng}
        self.cnt = {k: 0 for k in self.eng}
        self.seen = {k: {} for k in self.eng}
        self.free_ds = {"sw": [DSem(nc.alloc_semaphore("w%d" % i), "w%d" % i) for i in range(26)],
                        "hw": [DSem(nc.alloc_semaphore("d%d" % i), "d%d" % i) for i in range(46)]}
        self.used_ds = []
        self.nbank = 0

    def wait(self, eng, ev):
        key, h, v = ev
        if eng == "pe" and key == "pe":
            return
        if self.seen[eng].get(key, 0) >= v:
            return
        self.eng[eng].wait_ge(h, v)
        self.seen[eng][key] = v

    def _deps(self, eng, reads, writes):
        for b in reads:
            if b.w is not None:
                self.wait(eng, b.w)
        for b in writes:
            if b.w is not None:
                self.wait(eng, b.w)
            for ev in b.r.values():
                self.wait(eng, ev)

    def _mark(self, ev, reads, writes):
        for b in reads:
            b.r[ev[0]] = ev
        for b in writes:
            b.w = ev
            b.r = {}

    def op(self, eng, fn, reads=(), writes=()):
        self._deps(eng, reads, writes)
        ins = fn()
        self.cnt[eng] += 1
        ins.then_inc(self.sem[eng], 1)
        ev = (eng, self.sem[eng], self.cnt[eng])
        self._mark(ev, reads, writes)
        return ev

    def dma(self, q, out_ap, in_ap, owner, reads=(), writes=()):
        self._deps(q, reads, writes)
        kind = "sw" if q == "pool" else "hw"
        if owner.ds is None:
            owner.ds = self.free_ds[kind].pop()
            self.used_ds.append((owner, owner.ds, kind))
        ds = owner.ds
        if ds.last is not None:
            self.wait(q, ds.last)
        ins = self.eng[q].dma_start(out=out_ap, in_=in_ap)
        ds.cnt += 16
        ins.then_inc(ds.h, 16)
        ev = (ds.key, ds.h, ds.cnt)
        ds.last = ev
        self._mark(ev, reads, writes)
        return ev

    def barrier(self):
        for e in self.eng:
            for k in self.eng:
                if k != e and self.cnt[k] > 0:
                    self.wait(e, (k, self.sem[k], self.cnt[k]))
        for owner, ds, kind in self.used_ds:
            if ds.last is not None:
                for e in self.eng:
                    self.wait(e, ds.last)
            owner.ds = None
            self.free_ds[kind].append(ds)
        self.used_ds = []

    def mm(self, bankbuf, out_ap, pairs, reads, start=True, stop=True):
        nc = self.nc

        def fn():
            n = len(pairs)
            ins = None
            for i, (l, r) in enumerate(pairs):
                ins = nc.tensor.matmul(out_ap, lhsT=l, rhs=r, start=(start and i == 0), stop=(stop and i == n - 1))
            return ins

        return self.op("pe", fn, reads=reads, writes=[bankbuf])


def build_program():
    nc = bass.Bass("TRN2", target_bir_lowering=False)
    P = Prog(nc)

    def din(name, shape, dt=F32):
        return nc.dram_tensor(name, list(shape), dt, kind="ExternalInput").ap()

    def dscr(name, shape, dt=F32, out=False):
        return nc.dram_tensor(name, list(shape), dt, kind=("ExternalOutput" if out else "Internal")).ap()

    X0 = din("x0", [D, ALL])
    PT = din("pt", [256, OWN])
    WGU = [din("wgu%d" % i, [NJ, 128, 2 * 16 * 128]) for i in (1, 2)]
    WDN = [din("wdn%d" % i, [NPART, 8, 128, 2 * JP * 128]) for i in (1, 2)]
    WIN = din("win", [72, 128, 16 * 128])
    WROW = din("wrow", [2, 128, 16 * 1024])
    WA = din("wa", [16, 128, 8 * 128])
    WB = din("wb", [16, 128, 8 * 128])
    WO = din("wo", [16, 128, 16 * 128])
    WG = din("wg", [16, 128, 16 * 128])
    WP = din("wp", [16, 128, 2 * 128])
    LNG = din("lng", [4, 128, 16])
    LNB = din("lnb", [4, 128, 16])
    SGUG = din("sgug", [1, 1024])
    SGUB = din("sgub", [1, 1024])
    SGUWT = din("sguwt", [128, 8 * 128])
    SGUMASK = din("sgumask", [128, 128])
    SGUBIAS = din("sgubias", [1, 8 * 128])
    LAMV = din("lamv", [4, 64])
    ANG = din("ang", [128, 8])
    POSK = din("posk", [8, 4, ALL], BF16)
    POSQ = din("posq", [8, 4, OWN], BF16)
    DIAG = din("diag", [8, 128, 128], BF16)
    IDENT = din("ident", [128, 128], BF16)
    PMASK = din("pmask", [128, 128], BF16)

    dbg = STAGE < 99
    X1 = dscr("x1", [D, ALL], out=dbg)
    ACC = dscr("acc", [D, OWN])
    YS = dscr("ys", [D, OWN])
    QT = dscr("qt", [1024, OWN], BF16, out=dbg)
    KT = dscr("kt", [1024, ALL], BF16, out=dbg)
    VAL = dscr("val", [ALL, 1024], BF16, out=dbg)
    YA = dscr("ya", [1024, OWN], BF16, out=dbg)
    GA = dscr("ga", [D, OWN])
    GB = dscr("gb", [D, OWN])
    X2 = dscr("x2", [D, OWN], out=dbg)
    X3 = dscr("x3", [D, OWN], out=dbg)
    YBD = dscr("ybd", [1024, OWN], BF16, out=dbg)
    OUT = dscr("out", [D, OWN], out=True)

    top = ExitStack()
    with top:
        def sbt(es, name, shape, dt):
            return es.enter_context(nc.sbuf_tensor(name, list(shape), dt))

        banks = [top.enter_context(nc.psum_tensor("ps%d" % i, [128, 512], F32)) for i in range(8)]
        bankb = [Buf() for _ in range(8)]

        def bank():
            i = P.nbank % 8
            P.nbank += 1
            return i

        ones32 = sbt(top, "ones32", [128, 128], F32)
        onesb = sbt(top, "onesb", [128, 128], BF16)
        lng = sbt(top, "lng_s", [128, 4, 16], F32)
        lnb = sbt(top, "lnb_s", [128, 4, 16], F32)
        epsc = sbt(top, "epsc", [128, 1], F32)
        cb = Buf()
        P.op("dve", lambda: nc.vector.memset(ones32[:], 1.0), writes=[cb])
        P.op("dve", lambda: nc.vector.memset(onesb[:], 1.0), writes=[cb])
        P.op("dve", lambda: nc.vector.memset(epsc[:], LN_EPS), writes=[cb])
        lb1, lb2 = Buf(), Buf()
        P.dma("sp", lng[:], LNG.rearrange("l p f -> p l f"), lb1, writes=[lb1])
        P.dma("sp", lnb[:], LNB.rearrange("l p f -> p l f"), lb2, writes=[lb2])
        constb = [cb, lb1, lb2]

        def ln_pass(es, Ysrc, Xdst, tok0, s1, s2, s1b, s2b, li, tag, lin, linb, lout, loutb):
            mean = sbt(es, tag + "mean", [128, 512], F32)
            rstd = sbt(es, tag + "rstd", [128, 512], F32)
            nmr = sbt(es, tag + "nmr", [128, 512], F32)
            mb, rb, nb = Buf(), Buf(), Buf()
            for q in range(NQ):
                qs = slice(q * QW, (q + 1) * QW)
                b1 = bank()
                P.mm(bankb[b1], banks[b1][:], [(ones32[:], s1[:, qs])], reads=[s1b[q]] + constb)
                b2 = bank()
                P.mm(bankb[b2], banks[b2][:], [(ones32[:], s2[:, qs])], reads=[s2b[q]] + constb)
                P.op("dve", lambda: nc.vector.tensor_scalar(out=mean[:], in0=banks[b1][:], scalar1=1.0 / D, scalar2=None, op0=ALU.mult),
                     reads=[bankb[b1]], writes=[mb])
                P.op("dve", lambda: nc.vector.tensor_tensor(out=nmr[:], in0=mean[:], in1=mean[:], op=ALU.mult), reads=[mb], writes=[nb])
                P.op("dve", lambda: nc.vector.scalar_tensor_tensor(out=rstd[:], in0=banks[b2][:], scalar=1.0 / D, in1=nmr[:],
                                                                   op0=ALU.mult, op1=ALU.subtract), reads=[bankb[b2], nb], writes=[rb])
                P.op("act", lambda: nc.scalar.activation(out=rstd[:], in_=rstd[:], func=AF.Sqrt, bias=epsc[:, 0:1], scale=1.0),
                     reads=[rb] + constb, writes=[rb])
                P.op("dve", lambda: nc.vector.reciprocal(out=rstd[:], in_=rstd[:]), reads=[rb], writes=[rb])
                P.op("dve", lambda: nc.vector.scalar_tensor_tensor(out=nmr[:], in0=mean[:], scalar=-1.0, in1=rstd[:],
                                                                   op0=ALU.mult, op1=ALU.mult), reads=[mb, rb], writes=[nb])

                def load(f):
                    P.dma("sp", lin[f % 4][:], Ysrc[f * 128:(f + 1) * 128, qs], linb[f % 4],
                          reads=[ysb[(f, q)]], writes=[linb[f % 4]])

                for f in range(3):
                    load(f)
                for f in range(16):
                    if f + 3 < 16:
                        load(f + 3)
                    li_t, lo_t = lin[f % 4], lout[f % 3]
                    P.op("dve", lambda: nc.vector.tensor_tensor(out=li_t[:], in0=li_t[:], in1=rstd[:], op=ALU.mult),
                         reads=[linb[f % 4], rb], writes=[linb[f % 4]])
                    P.op("dve", lambda: nc.vector.tensor_tensor(out=li_t[:], in0=li_t[:], in1=nmr[:], op=ALU.add),
                         reads=[linb[f % 4], nb], writes=[linb[f % 4]])
                    P.op("act", lambda: nc.scalar.activation(out=lo_t[:], in_=li_t[:], func=AF.Identity,
                                                             bias=lnb[:, li, f:f + 1], scale=lng[:, li, f:f + 1]),
                         reads=[linb[f % 4]] + constb, writes=[loutb[f % 3]])
                    P.dma("act", Xdst[f * 128:(f + 1) * 128, tok0 + q * QW: tok0 + (q + 1) * QW], lo_t[:], loutb[f % 3],
                          reads=[loutb[f % 3]])

        ysb = {(f, q): Buf() for f in range(16) for q in range(NQ)}
        accb = {(f, q): Buf() for f in range(16) for q in range(NQ)}

        def stats_update(y_t, yb, s1, s2, s1b, s2b, q, sq_t, sqb, first):
            qs = slice(q * QW, (q + 1) * QW)
            if first:
                P.op("dve", lambda: nc.vector.tensor_copy(out=s1[:, qs], in_=y_t[:]), reads=[yb], writes=[s1b[q]])
                P.op("act", lambda: nc.scalar.activation(out=s2[:, qs], in_=y_t[:], func=AF.Square), reads=[yb], writes=[s2b[q]])
            else:
                P.op("dve", lambda: nc.vector.tensor_tensor(out=s1[:, qs], in0=s1[:, qs], in1=y_t[:], op=ALU.add),
                     reads=[yb, s1b[q]], writes=[s1b[q]])
                P.op("act", lambda: nc.scalar.activation(out=sq_t[:], in_=y_t[:], func=AF.Square), reads=[yb], writes=[sqb])
                P.op("dve", lambda: nc.vector.tensor_tensor(out=s2[:, qs], in0=s2[:, qs], in1=sq_t[:], op=ALU.add),
                     reads=[sqb, s2b[q]], writes=[s2b[q]])

        def ln_pipeline(es, tag, Xout, tok0, li, produce, prefetch, depth, lout, loutb):
            ybuf = sbt(es, tag + "ybuf", [128, 16, 512], F32)
            ybb = [Buf() for _ in range(16)]
            s1 = sbt(es, tag + "ls1", [128, 512], F32)
            s2 = sbt(es, tag + "ls2", [128, 512], F32)
            s1b, s2b = Buf(), Buf()
            sqt = [sbt(es, tag + "lsq%d" % i, [128, 512], F32) for i in range(2)]
            sqb = [Buf() for _ in range(2)]
            mean = sbt(es, tag + "lmean", [128, 512], F32)
            mb = Buf()
            rstd = [sbt(es, tag + "lrstd%d" % i, [128, 512], F32) for i in range(2)]
            nmr = [sbt(es, tag + "lnmr%d" % i, [128, 512], F32) for i in range(2)]
            rb = [Buf() for _ in range(2)]
            nb = [Buf() for _ in range(2)]
            nlo = len(lout)

            def stats_fin(q):
                par = q % 2
                flush_s2()
                b1 = bank()
                P.mm(bankb[b1], banks[b1][:], [(ones32[:], s1[:])], reads=[s1b] + constb)
                b2 = bank()
                P.mm(bankb[b2], banks[b2][:], [(ones32[:], s2[:])], reads=[s2b] + constb)
                P.op("dve", lambda: nc.vector.tensor_scalar(out=mean[:], in0=banks[b1][:], scalar1=1.0 / D, scalar2=None, op0=ALU.mult),
                     reads=[bankb[b1]], writes=[mb])
                P.op("dve", lambda: nc.vector.tensor_tensor(out=nmr[par][:], in0=mean[:], in1=mean[:], op=ALU.mult), reads=[mb], writes=[nb[par]])
                P.op("dve", lambda: nc.vector.scalar_tensor_tensor(out=rstd[par][:], in0=banks[b2][:], scalar=1.0 / D, in1=nmr[par][:],
                                                                   op0=ALU.mult, op1=ALU.subtract), reads=[bankb[b2], nb[par]], writes=[rb[par]])
                P.op("act", lambda: nc.scalar.activation(out=rstd[par][:], in_=rstd[par][:], func=AF.Ln, bias=epsc[:, 0:1], scale=1.0),
                     reads=[rb[par]] + constb, writes=[rb[par]])
                P.op("act", lambda: nc.scalar.activation(out=rstd[par][:], in_=rstd[par][:], func=AF.Exp, scale=-0.5),
                     reads=[rb[par]], writes=[rb[par]])
                P.op("dve", lambda: nc.vector.scalar_tensor_tensor(out=nmr[par][:], in0=mean[:], scalar=-1.0, in1=rstd[par][:],
                                                                   op0=ALU.mult, op1=ALU.mult), reads=[mb, rb[par]], writes=[nb[par]])

            def ln_tile(q, f, meng="pool"):
                par = q % 2
                k = (q * 16 + f) % nlo
                y = ybuf[:, f, :]
                me = nc.gpsimd if meng == "pool" else nc.vector
                P.op(meng, lambda: me.tensor_tensor(out=y, in0=y, in1=rstd[par][:], op=ALU.mult),
                     reads=[ybb[f], rb[par]], writes=[ybb[f]])
                aeng = "pool" if (meng == "pool" and f % 2 == 1) else "dve"
                ae = nc.gpsimd if aeng == "pool" else nc.vector
                P.op(aeng, lambda: ae.tensor_tensor(out=y, in0=y, in1=nmr[par][:], op=ALU.add),
                     reads=[ybb[f], nb[par]], writes=[ybb[f]])
                P.op("act", lambda: nc.scalar.activation(out=lout[k][:], in_=y, func=AF.Identity,
                                                         bias=lnb[:, li, f:f + 1], scale=lng[:, li, f:f + 1]),
                     reads=[ybb[f]] + constb, writes=[loutb[k]])
                P.dma("act", Xout[f * 128:(f + 1) * 128, tok0 + q * QW: tok0 + (q + 1) * QW], lout[k][:], loutb[k], reads=[loutb[k]])

            pend = []

            def flush_s2():
                while pend:
                    f0 = pend.pop(0)
                    P.op("dve", lambda: nc.vector.tensor_tensor(out=s2[:], in0=s2[:], in1=sqt[f0 % 2][:], op=ALU.add),
                         reads=[sqb[f0 % 2], s2b], writes=[s2b])

            def stats_upd(f):
                y = ybuf[:, f, :]
                if f == 0:
                    P.op("dve", lambda: nc.vector.tensor_copy(out=s1[:], in_=y), reads=[ybb[f]], writes=[s1b])
                    P.op("act", lambda: nc.scalar.activation(out=s2[:], in_=y, func=AF.Square), reads=[ybb[f]], writes=[s2b])
                else:
                    P.op("dve", lambda: nc.vector.tensor_tensor(out=s1[:], in0=s1[:], in1=y, op=ALU.add), reads=[ybb[f], s1b], writes=[s1b])
                    P.op("act", lambda: nc.scalar.activation(out=sqt[f % 2][:], in_=y, func=AF.Square), reads=[ybb[f]], writes=[sqb[f % 2]])
                    flush_s2()
                    pend.append(f)

            for i0 in range(depth):
                prefetch(i0)
            for q in range(NQ):
                for f in range(16):
                    i = q * 16 + f
                    if i + depth < NQ * 16:
                        prefetch(i + depth)
                    if q > 0:
                        if f == 0:
                            ln_tile(q - 1, 0)
                            ln_tile(q - 1, 1)
                        if f + 2 < 16:
                            ln_tile(q - 1, f + 2)
                    produce(q, f, i, ybuf[:, f, :], ybb[f])
                    stats_upd(f)
                stats_fin(q)
            for f in range(16):
                ln_tile(NQ - 1, f, "dve" if f % 2 == 0 else "pool")

        def ffn_phase(Xin, Xout, tok0, Wgu_d, Wd_d, li, tag):
            with ExitStack() as es:
                xT = sbt(es, tag + "xT", [128, 16, OWN], BF16)
                h = sbt(es, tag + "h", [128, JP, OWN], BF16)
                wgu = [sbt(es, tag + "wgu%d" % i, [128, 2, 16, 128], BF16) for i in range(2)]
                wd = [sbt(es, tag + "wd%d" % i, [128, 2, JP, 128], BF16) for i in range(2)]
                sg = [sbt(es, tag + "sg%d" % i, [128, 512], F32) for i in range(2)]
                sin = [sbt(es, tag + "sin%d" % i, [128, 512], F32) for i in range(4)]
                sout = [sbt(es, tag + "sout%d" % i, [128, 512], F32) for i in range(3)]
                xTb = [Buf() for _ in range(NQ)]
                hb = [[Buf() for _ in range(NQ)] for _ in range(JP)]
                wgub = [Buf() for _ in range(2)]
                wdb = [Buf() for _ in range(2)]
                sgb = [Buf() for _ in range(2)]
                sinb = [Buf() for _ in range(4)]
                soutb = [Buf() for _ in range(3)]
                def load_xq(q):
                    P.dma("pool", xT[:, :, q * QW:(q + 1) * QW],
                          Xin[:, tok0 + q * QW: tok0 + (q + 1) * QW].rearrange("(f p) t -> p f t", p=128),
                          xTb[q], writes=[xTb[q]])
                load_xq(0)
                ngu = 0
                nwd = [0]
                nsg = 0
                for p in range(NPART):
                    if p == NPART - 1:
                        for s_ in range(2):
                            P.dma("pool", wd[s_][:], Wd_d[p, s_].rearrange("p (f j c) -> p f j c", f=2, j=JP), wdb[s_], writes=[wdb[s_]])
                    for jj in range(JP):
                        j = p * JP + jj
                        sl = ngu % 2
                        ngu += 1
                        P.dma("pool", wgu[sl][:], Wgu_d[j].rearrange("p (s k c) -> p s k c", s=2, k=16), wgub[sl], writes=[wgub[sl]])
                        if j == 0:
                            for q in range(1, NQ):
                                load_xq(q)
                        for q in range(NQ):
                            qs = slice(q * QW, (q + 1) * QW)
                            bg = bank()
                            P.mm(bankb[bg], banks[bg][:], [(wgu[sl][:, 0, kc, :], xT[:, kc, qs]) for kc in range(16)],
                                 reads=[wgub[sl], xTb[q]])
                            bu = bank()
                            P.mm(bankb[bu], banks[bu][:], [(wgu[sl][:, 1, kc, :], xT[:, kc, qs]) for kc in range(16)],
                                 reads=[wgub[sl], xTb[q]])
                            st = nsg % 2
                            nsg += 1
                            P.op("act", lambda: nc.scalar.activation(out=sg[st][:], in_=banks[bg][:], func=AF.Silu),
                                 reads=[bankb[bg]], writes=[sgb[st]])
                            P.op("dve", lambda: nc.vector.tensor_tensor(out=h[:, jj, qs], in0=banks[bu][:], in1=sg[st][:], op=ALU.mult),
                                 reads=[bankb[bu], sgb[st]], writes=[hb[jj][q]])
                    if p == NPART - 1:
                        break
                    items = [(f, q) for f in range(16) for q in range(NQ)]

                    def load(i):
                        f, q = items[i]
                        src = Xin if p == 0 else ACC
                        t0 = tok0 if p == 0 else 0
                        rd = [] if p == 0 else [accb[(f, q)]]
                        P.dma("sp", sin[i % 4][:], src[f * 128:(f + 1) * 128, t0 + q * QW: t0 + (q + 1) * QW], sinb[i % 4],
                              reads=rd, writes=[sinb[i % 4]])

                    for i in range(3):
                        load(i)
                    for i, (f, q) in enumerate(items):
                        qs = slice(q * QW, (q + 1) * QW)
                        fg, fi = f // 2, f % 2
                        if fi == 0 and q == 0:
                            sl = nwd[0] % 2
                            nwd[0] += 1
                            P.dma("pool", wd[sl][:], Wd_d[p, fg].rearrange("p (f j c) -> p f j c", f=2, j=JP), wdb[sl], writes=[wdb[sl]])
                        if i + 3 < len(items):
                            load(i + 3)
                        bk = bank()
                        P.mm(bankb[bk], banks[bk][:], [(wd[sl][:, fi, jj, :], h[:, jj, qs]) for jj in range(JP)],
                             reads=[wdb[sl]] + [hb[jj][q] for jj in range(JP)])
                        si, so = sin[i % 4], sout[i % 3]
                        if p == 0:
                            P.op("act", lambda: nc.scalar.mul(out=si[:], in_=si[:], mul=ALPHA), reads=[sinb[i % 4]], writes=[sinb[i % 4]])
                        P.op("dve", lambda: nc.vector.scalar_tensor_tensor(out=so[:], in0=banks[bk][:], scalar=0.5, in1=si[:],
                                                                           op0=ALU.mult, op1=ALU.add),
                             reads=[bankb[bk], sinb[i % 4]], writes=[soutb[i % 3]])
                        P.dma("sp", ACC[f * 128:(f + 1) * 128, qs], so[:], soutb[i % 3], reads=[soutb[i % 3]], writes=[accb[(f, q)]])

                pl = NPART - 1
                wdl = xT[:, 0:JP, :].rearrange("p a b -> p (a b)").rearrange("p (f j c) -> p f j c", f=16, j=JP)
                wdlb = [Buf() for _ in range(8)]
                for fg in range(2, 8):
                    P.dma("pool", wdl[:, 2 * fg:2 * fg + 2, :, :], Wd_d[pl, fg].rearrange("p (f j c) -> p f j c", f=2, j=JP),
                          wdlb[fg], writes=[wdlb[fg]] + (xTb if fg == 2 else []))

                def prefetch(i):
                    q, f = divmod(i, 16)
                    P.dma("sp", sin[i % 4][:], ACC[f * 128:(f + 1) * 128, q * QW:(q + 1) * QW], sinb[i % 4],
                          reads=[accb[(f, q)]], writes=[sinb[i % 4]])

                def produce(q, f, i, ydst, ydb):
                    qs = slice(q * QW, (q + 1) * QW)
                    bk = bank()
                    if f < 4:
                        P.mm(bankb[bk], banks[bk][:], [(wd[f // 2][:, f % 2, jj, :], h[:, jj, qs]) for jj in range(JP)],
                             reads=[wdb[f // 2]] + [hb[jj][q] for jj in range(JP)])
                    else:
                        P.mm(bankb[bk], banks[bk][:], [(wdl[:, f, jj, :], h[:, jj, qs]) for jj in range(JP)],
                             reads=[wdlb[f // 2]] + [hb[jj][q] for jj in range(JP)])
                    P.op("dve", lambda: nc.vector.scalar_tensor_tensor(out=ydst, in0=banks[bk][:], scalar=0.5, in1=sin[i % 4][:],
                                                                       op0=ALU.mult, op1=ALU.add),
                         reads=[bankb[bk], sinb[i % 4]], writes=[ydb])

                ln_pipeline(es, tag, Xout, tok0, li, produce, prefetch, 3, sout, soutb)
                P.barrier()

        def resid_tile(i, f, q, src_ap, src_reads, sin, sinb, sout, soutb, sqt, sqb, s1, s2, s1b, s2b):
            qs = slice(q * QW, (q + 1) * QW)
            si, so = sin[i % 4], sout[i % 3]
            P.op("act", lambda: nc.scalar.mul(out=si[:], in_=si[:], mul=ALPHA), reads=[sinb[i % 4]], writes=[sinb[i % 4]])
            P.op("dve", lambda: nc.vector.tensor_tensor(out=so[:], in0=src_ap, in1=si[:], op=ALU.add),
                 reads=list(src_reads) + [sinb[i % 4]], writes=[soutb[i % 3]])
            P.dma("sp", YS[f * 128:(f + 1) * 128, qs], so[:], soutb[i % 3], reads=[soutb[i % 3]], writes=[ysb[(f, q)]])
            stats_update(so, soutb[i % 3], s1, s2, s1b, s2b, q, sqt[i % 2], sqb[i % 2], f == 0)

        def epi_bufs(es, tag):
            sin = [sbt(es, tag + "sin%d" % i, [128, 512], F32) for i in range(4)]
            sout = [sbt(es, tag + "sout%d" % i, [128, 512], F32) for i in range(3)]
            sqt = [sbt(es, tag + "sq%d" % i, [128, 512], F32) for i in range(2)]
            s1 = sbt(es, tag + "s1", [128, OWN], F32)
            s2 = sbt(es, tag + "s2", [128, OWN], F32)
            return (sin, [Buf() for _ in range(4)], sout, [Buf() for _ in range(3)], sqt, [Buf() for _ in range(2)],
                    s1, s2, [Buf() for _ in range(NQ)], [Buf() for _ in range(NQ)])

        def proj_phase():
            with ExitStack() as es:
                tag = "pj"
                xT = sbt(es, tag + "xT", [128, 16, OWN], BF16)
                wrow = sbt(es, tag + "wrow", [128, 16, 1024], BF16)
                VN = sbt(es, tag + "VN", [128, 16, 1024], BF16)
                wch = [sbt(es, tag + "wch%d" % i, [128, 16, 128], BF16) for i in range(3)]
                gbc = sbt(es, tag + "gbc", [128, 1024], F32)
                bbc = sbt(es, tag + "bbc", [128, 1024], F32)
                wst = sbt(es, tag + "wst", [128, 8, 128], F32)
                wsb = sbt(es, tag + "wsb", [128, 8, 128], BF16)
                msk = sbt(es, tag + "msk", [128, 128], F32)
                sbias = sbt(es, tag + "sbias", [128, 1024], F32)
                vt = [sbt(es, tag + "vt%d" % i, [128, 1024], F32) for i in range(2)]
                vsq = sbt(es, tag + "vsq", [128, 1024], F32)
                sm = sbt(es, tag + "sm", [128, 8], F32)
                ut = [sbt(es, tag + "ut%d" % i, [128, 512], F32) for i in range(2)]
                stmp = [sbt(es, tag + "stmp%d" % i, [128, 512], F32) for i in range(2)]
                stb = [sbt(es, tag + "stb%d" % i, [128, 512], BF16) for i in range(3)]
                stf = [sbt(es, tag + "stf%d" % i, [128, 512], F32) for i in range(3)]
                vst = [sbt(es, tag + "vst%d" % i, [128, 1024], BF16) for i in range(2)]
                xTb = [Buf() for _ in range(NQ)]
                wrowb = Buf()
                VNb = [Buf() for _ in range(16)]
                wchb = [Buf() for _ in range(3)]
                cst = [Buf() for _ in range(5)]
                vtb = [Buf() for _ in range(2)]
                vsqb, smb = Buf(), Buf()
                utb = [Buf() for _ in range(2)]
                stmpb = [Buf() for _ in range(2)]
                stbb = [Buf() for _ in range(3)]
                stfb = [Buf() for _ in range(3)]
                vstb = [Buf() for _ in range(2)]

                def load_xq(tok0, q):
                    P.dma("pool", xT[:, :, q * QW:(q + 1) * QW],
                          X1[:, tok0 + q * QW: tok0 + (q + 1) * QW].rearrange("(f p) t -> p f t", p=128),
                          xTb[q], writes=[xTb[q]])

                def load_xT(tok0):
                    for q in range(NQ):
                        load_xq(tok0, q)

                def load_wrow_part(which, k4):
                    P.dma("pool", wrow[:, k4 * 4:(k4 + 1) * 4, :],
                          WROW[which][:, k4 * 4096:(k4 + 1) * 4096].rearrange("p (k c) -> p k c", k=4),
                          wrowb, writes=[wrowb])

                def load_wrow(which):
                    for k4 in range(4):
                        load_wrow_part(which, k4)

                load_xq(0, 0)
                extra = [lambda q=q: load_xq(0, q) for q in range(1, NQ)] + [lambda k4=k4: load_wrow_part(0, k4) for k4 in range(4)]
                P.dma("sp", gbc[:], SGUG[0:1, :].partition_broadcast(128), cst[0], writes=[cst[0]])
                P.dma("sp", bbc[:], SGUB[0:1, :].partition_broadcast(128), cst[1], writes=[cst[1]])
                P.dma("sp", wst[:], SGUWT.rearrange("s (g t) -> s g t", g=8), cst[2], writes=[cst[2]])
                P.dma("sp", msk[:], SGUMASK, cst[3], writes=[cst[3]])
                P.dma("sp", sbias[:], SGUBIAS[0:1, :].partition_broadcast(128), cst[4], writes=[cst[4]])
                for g in range(8):
                    P.op("dve", lambda: nc.vector.tensor_tensor(out=wsb[:, g, :], in0=wst[:, g, :], in1=msk[:], op=ALU.mult),
                         reads=[cst[2], cst[3]], writes=[cst[2]] if g == 7 else [])
                wsbb = cst[2]

                def rowproj(blk, evac):
                    for half in range(2):
                        bk = bank()
                        P.mm(bankb[bk], banks[bk][:],
                             [(xT[:, kc, blk * 128:(blk + 1) * 128], wrow[:, kc, half * 512:(half + 1) * 512]) for kc in range(16)],
                             reads=[xTb[blk // 4], wrowb])
                        evac(half, bk)

                nch = [0]
                nst = [0]

                def chunk(c, per_q):
                    sl = nch[0] % 3
                    nch[0] += 1
                    P.dma("pool", wch[sl][:], WIN[c].rearrange("p (k c) -> p k c", k=16), wchb[sl], writes=[wchb[sl]])
                    if extra:
                        nx = 3 if len(extra) == 7 else 1
                        for _ in range(nx):
                            extra.pop(0)()
                    for q in range(NQ):
                        qs = slice(q * QW, (q + 1) * QW)
                        bk = bank()
                        P.mm(bankb[bk], banks[bk][:], [(wch[sl][:, kc, :], xT[:, kc, qs]) for kc in range(16)],
                             reads=[wchb[sl], xTb[q]])
                        per_q(q, qs, bk)

                def qk_chunk(c, dst, row0, tok0, scale):
                    def per_q(q, qs, bk):
                        k3 = nst[0] % 3
                        nst[0] += 1
                        P.op("act", lambda: nc.scalar.mul(out=stb[k3][:], in_=banks[bk][:], mul=scale),
                             reads=[bankb[bk]], writes=[stbb[k3]])
                        P.dma("act", dst[row0:row0 + 128, tok0 + q * QW: tok0 + (q + 1) * QW], stb[k3][:], stbb[k3], reads=[stbb[k3]])
                    chunk(c, per_q)

                for c in range(8):
                    qk_chunk(16 + c, QT, c * 128, 0, 0.125)
                for c in range(8):
                    qk_chunk(24 + c, KT, c * 128, 0, 1.0)
                for blk in range(16):
                    v_t = vt[blk % 2]
                    vb = vtb[blk % 2]

                    def ev(half, bk):
                        P.op("act", lambda: nc.scalar.activation(out=v_t[:, half * 512:(half + 1) * 512], in_=banks[bk][:],
                                                                 func=AF.Gelu_apprx_tanh),
                             reads=[bankb[bk]], writes=[vb])
                    rowproj(blk, ev)
                    P.op("dve", lambda: nc.vector.reduce_sum(out=sm[:, 0:1], in_=v_t[:], axis=mybir.AxisListType.X), reads=[vb], writes=[smb])
                    P.op("dve", lambda: nc.vector.tensor_tensor(out=vsq[:], in0=v_t[:], in1=v_t[:], op=ALU.mult), reads=[vb], writes=[vsqb])
                    P.op("dve", lambda: nc.vector.reduce_sum(out=sm[:, 1:2], in_=vsq[:], axis=mybir.AxisListType.X), reads=[vsqb], writes=[smb])
                    P.op("dve", lambda: nc.vector.tensor_scalar(out=sm[:, 2:4], in0=sm[:, 0:2], scalar1=1.0 / 1024, scalar2=None, op0=ALU.mult),
                         reads=[smb], writes=[smb])
                    P.op("dve", lambda: nc.vector.tensor_tensor(out=sm[:, 4:5], in0=sm[:, 2:3], in1=sm[:, 2:3], op=ALU.mult), reads=[smb], writes=[smb])
                    P.op("dve", lambda: nc.vector.tensor_tensor(out=sm[:, 5:6], in0=sm[:, 3:4], in1=sm[:, 4:5], op=ALU.subtract), reads=[smb], writes=[smb])
                    P.op("act", lambda: nc.scalar.activation(out=sm[:, 6:7], in_=sm[:, 5:6], func=AF.Sqrt, bias=epsc[:, 0:1], scale=1.0),
                         reads=[smb] + constb, writes=[smb])
                    P.op("dve", lambda: nc.vector.reciprocal(out=sm[:, 7:8], in_=sm[:, 6:7]), reads=[smb], writes=[smb])
                    P.op("dve", lambda: nc.vector.tensor_scalar(out=v_t[:], in0=v_t[:], scalar1=sm[:, 2:3], scalar2=sm[:, 7:8],
                                                                op0=ALU.subtract, op1=ALU.mult), reads=[vb, smb], writes=[vb])
                    P.op("dve", lambda: nc.vector.tensor_tensor(out=v_t[:], in0=v_t[:], in1=gbc[:], op=ALU.mult), reads=[vb, cst[0]], writes=[vb])
                    P.op("dve", lambda: nc.vector.tensor_tensor(out=VN[:, blk, :], in0=v_t[:], in1=bbc[:], op=ALU.add),
                         reads=[vb, cst[1]], writes=[VNb[blk]])

                for g in range(8):
                    def per_q(q, qs, bk, g=g):
                        k2 = nst[0] % 2
                        nst[0] += 1
                        P.op("act", lambda: nc.scalar.activation(out=ut[k2][:], in_=banks[bk][:], func=AF.Gelu_apprx_tanh),
                             reads=[bankb[bk]], writes=[utb[k2]])
                        bs = bank()

                        def fn():
                            ins = None
                            for k in range(4):
                                ins = nc.tensor.matmul(banks[bs][:, k * 128:(k + 1) * 128], lhsT=VN[:, 4 * q + k, g * 128:(g + 1) * 128],
                                                       rhs=wsb[:, g, :], start=True, stop=True)
                            return ins
                        P.op("pe", fn, reads=[VNb[4 * q + k] for k in range(4)] + [wsbb], writes=[bankb[bs]])
                        for k in range(4):
                            ks = slice(k * 128, (k + 1) * 128)
                            P.op("dve", lambda: nc.vector.tensor_tensor(out=stmp[k2][:, ks], in0=banks[bs][:, ks], in1=sbias[:, g * 128:(g + 1) * 128], op=ALU.add),
                                 reads=[bankb[bs], cst[4]], writes=[stmpb[k2]])
                        k3 = (nst[0] - 1) % 3
                        P.op("dve", lambda: nc.vector.tensor_tensor(out=stb[k3][:], in0=stmp[k2][:], in1=ut[k2][:], op=ALU.mult),
                             reads=[stmpb[k2], utb[k2]], writes=[stbb[k3]])
                        P.dma("sp", YA[g * 128:(g + 1) * 128, qs], stb[k3][:], stbb[k3], reads=[stbb[k3]])
                    chunk(g, per_q)

                for c in range(32):
                    dst = GA if c < 16 else GB
                    row0 = (c % 16) * 128

                    def per_q(q, qs, bk, dst=dst, row0=row0):
                        k3 = nst[0] % 3
                        nst[0] += 1
                        P.op("act", lambda: nc.scalar.activation(out=stf[k3][:], in_=banks[bk][:], func=AF.Sigmoid),
                             reads=[bankb[bk]], writes=[stfb[k3]])
                        P.dma("act", dst[row0:row0 + 128, qs], stf[k3][:], stfb[k3], reads=[stfb[k3]])
                    chunk(40 + c, per_q)

                def val_blocks(tok0):
                    for blk in range(16):
                        vs = vst[blk % 2]
                        vsb_ = vstb[blk % 2]

                        def ev(half, bk):
                            P.op("dve", lambda: nc.vector.tensor_copy(out=vs[:, half * 512:(half + 1) * 512], in_=banks[bk][:]),
                                 reads=[bankb[bk]], writes=[vsb_])
                        rowproj(blk, ev)
                        P.dma("sp", VAL[tok0 + blk * 128: tok0 + (blk + 1) * 128, :], vs[:], vsb_, reads=[vsb_])

                load_wrow(1)
                val_blocks(0)
                load_xT(OWN)
                for c in range(8):
                    qk_chunk(24 + c, KT, c * 128, OWN, 1.0)
                val_blocks(OWN)
                P.barrier()

        def attn_phase():
            with ExitStack() as es:
                tag = "at"
                KA = [[sbt(es, tag + "ka%d%d" % (s, i), [128, ALL], BF16) for i in range(2)] for s in range(2)]
                QA = [[sbt(es, tag + "qa%d%d" % (s, i), [128, OWN], BF16) for i in range(2)] for s in range(2)]
                VH = [sbt(es, tag + "vh%d" % i, [128, 32, 128], BF16) for i in range(2)]
                DG = [sbt(es, tag + "dg%d" % i, [128, 128], BF16) for i in range(2)]
                ident = sbt(es, tag + "ident", [128, 128], BF16)
                pt = [sbt(es, tag + "pt%d" % i, [128, 512], BF16) for i in range(4)]
                ctmp = [sbt(es, tag + "ct%d" % i, [128, 128], F32) for i in range(2)]
                o1 = [sbt(es, tag + "o1%d" % i, [128, 512], F32) for i in range(2)]
                o2 = [sbt(es, tag + "o2%d" % i, [128, 512], F32) for i in range(2)]
                d1 = [sbt(es, tag + "d1%d" % i, [128, 512], F32) for i in range(2)]
                d2 = [sbt(es, tag + "d2%d" % i, [128, 512], F32) for i in range(2)]
                sqo = [sbt(es, tag + "sqo%d" % i, [128, 512], F32) for i in range(2)]
                rst = [sbt(es, tag + "rst%d" % i, [128, 512], F32) for i in range(2)]
                ybs = [sbt(es, tag + "ybs%d" % i, [128, 512], BF16) for i in range(2)]
                lamt = sbt(es, tag + "lamt", [128, 4, 64], F32)
                lamp = sbt(es, tag + "lamp", [128, 64], F32)
                lsm = sbt(es, tag + "lsm", [128, 8], F32)
                ang = sbt(es, tag + "ang", [128, 8], F32)
                pmask = sbt(es, tag + "pmask", [128, 128], BF16)
                eps128 = sbt(es, tag + "eps128", [128, 1], F32)
                KAb = [[Buf() for _ in range(2)] for _ in range(2)]
                QAb = [[Buf() for _ in range(2)] for _ in range(2)]
                VHb = [Buf() for _ in range(2)]
                DGb = [Buf() for _ in range(2)]
                ptb = [Buf() for _ in range(4)]
                ctb = [Buf() for _ in range(2)]
                o1b = [Buf() for _ in range(2)]
                o2b = [Buf() for _ in range(2)]
                d1b = [Buf() for _ in range(2)]
                d2b = [Buf() for _ in range(2)]
                sqob = [Buf() for _ in range(2)]
                rstb = [Buf() for _ in range(2)]
                ybsb = [Buf() for _ in range(2)]
                lamb, angb, pmb = Buf(), Buf(), Buf()
                lampb, lsmb = Buf(), Buf()
                for k in range(4):
                    P.dma("sp", lamt[:, k, :], LAMV[k:k + 1, :].partition_broadcast(128), lamb, writes=[lamb])
                P.dma("sp", ang[:], ANG, angb, writes=[angb])
                P.dma("sp", pmask[:], PMASK, pmb, writes=[pmb])
                identb = Buf()
                P.dma("sp", ident[:], IDENT, identb, writes=[identb])
                epsb = Buf()
                P.op("dve", lambda: nc.vector.memset(eps128[:], LN_EPS), writes=[epsb])
                for k in range(2):
                    P.op("dve", lambda: nc.vector.tensor_tensor(out=lamp[:], in0=lamt[:, 2 * k, :], in1=lamt[:, 2 * k + 1, :], op=ALU.mult),
                         reads=[lamb], writes=[lampb])
                    P.op("dve", lambda: nc.vector.reduce_sum(out=lsm[:, k:k + 1], in_=lamp[:], axis=mybir.AxisListType.X), reads=[lampb], writes=[lsmb])
                P.op("act", lambda: nc.scalar.activation(out=lsm[:, 2:4], in_=lsm[:, 0:2], func=AF.Exp), reads=[lsmb], writes=[lsmb])
                P.op("dve", lambda: nc.vector.tensor_tensor(out=lsm[:, 4:5], in0=lsm[:, 2:3], in1=lsm[:, 3:4], op=ALU.subtract), reads=[lsmb], writes=[lsmb])
                P.op("dve", lambda: nc.vector.tensor_scalar(out=lsm[:, 5:6], in0=lsm[:, 4:5], scalar1=LAM_INIT, scalar2=-1.0, op0=ALU.add, op1=ALU.mult),
                     reads=[lsmb], writes=[lsmb])
                P.op("dve", lambda: nc.vector.tensor_scalar(out=ang[:], in0=ang[:], scalar1=(1.0 - LAM_INIT), scalar2=None, op0=ALU.mult),
                     reads=[angb], writes=[angb])
                SB = [0, 1, 2, 5]
                NSB = 4
                O = [3, 4]
                Dn = [6, 7]
                MS = 6
                dacc = [[sbt(es, tag + "dacc%d%d" % (s_, t_), [128, 512], F32) for t_ in range(2)] for s_ in range(2)]
                daccb = [[Buf() for _ in range(2)] for _ in range(2)]

                def load_head(h):
                    hb = h % 2
                    for s in range(2):
                        r0 = h * 128 + s * 64
                        P.dma("sp", KA[s][hb][0:64, :], KT[r0:r0 + 64, :], KAb[s][hb], writes=[KAb[s][hb]])
                        P.dma("sp", KA[s][hb][64:68, :], POSK[h], KAb[s][hb], writes=[KAb[s][hb]])
                        P.dma("sp", QA[s][hb][0:64, :], QT[r0:r0 + 64, :], QAb[s][hb], writes=[QAb[s][hb]])
                        P.dma("sp", QA[s][hb][64:68, :], POSQ[h], QAb[s][hb], writes=[QAb[s][hb]])
                    P.dma("sp", VH[hb][:], VAL[:, h * 128:(h + 1) * 128].rearrange("(n p) e -> p n e", p=128), VHb[hb], writes=[VHb[hb]])
                    P.dma("sp", DG[hb][:], DIAG[h], DGb[hb], writes=[DGb[hb]])

                units = []
                for h in range(8):
                    for i in range(NQ):
                        blocks = []
                        for jb in range(4 * i):
                            blocks.append((jb, 0, "full"))
                        for jb in range(4 * i):
                            blocks.append((16 + jb, 0, "full"))
                        for m in range(4):
                            blocks.append((4 * i + m, m * 128, "own"))
                        for m in range(4):
                            blocks.append((16 + 4 * i + m, m * 128, "oth"))
                        assert blocks[0][1] == 0
                        nblk = len(blocks)
                        for bi, (kb, c0, kind) in enumerate(blocks):
                            for s in range(2):
                                units.append((h, i, bi, nblk, kb, c0, kind, s))
                NU = len(units)
                LOOK = 3

                def emit_S(k):
                    h, i, bi, nblk, kb, c0, kind, s = units[k]
                    hb = h % 2
                    sb_ = SB[k % NSB]
                    q0 = i * QW
                    if kind == "full":
                        P.mm(bankb[sb_], banks[sb_][:, c0:512],
                             [(KA[s][hb][0:68, kb * 128:(kb + 1) * 128], QA[s][hb][0:68, q0 + c0:q0 + 512])],
                             reads=[KAb[s][hb], QAb[s][hb]])
                    else:
                        fix = DG[hb] if kind == "own" else pmask
                        fixb = DGb[hb] if kind == "own" else pmb

                        def fn():
                            nc.tensor.matmul(banks[sb_][:, c0:512], lhsT=KA[s][hb][0:68, kb * 128:(kb + 1) * 128],
                                             rhs=QA[s][hb][0:68, q0 + c0:q0 + 512], start=True, stop=False)
                            return nc.tensor.matmul(banks[sb_][:, c0:c0 + 128], lhsT=ident[:], rhs=fix[:], start=False, stop=True)
                        P.op("pe", fn, reads=[KAb[s][hb], QAb[s][hb], fixb, identb], writes=[bankb[sb_]])

                pending = []

                def fin_a(h, i):
                    t = (h * NQ + i) % 2
                    P.op("act", lambda: nc.scalar.copy(out=o1[t][:], in_=banks[O[0]][:]), reads=[bankb[O[0]]], writes=[o1b[t]])
                    P.op("act", lambda: nc.scalar.copy(out=o2[t][:], in_=banks[O[1]][:]), reads=[bankb[O[1]]], writes=[o2b[t]])
                    P.mm(bankb[Dn[0]], banks[Dn[0]][:], [(ones32[:], dacc[0][t][:])], reads=[daccb[0][t]] + constb)
                    P.mm(bankb[Dn[1]], banks[Dn[1]][:], [(ones32[:], dacc[1][t][:])], reads=[daccb[1][t]] + constb)
                    P.op("act", lambda: nc.scalar.activation(out=d1[t][:], in_=banks[Dn[0]][:], func=AF.Ln), reads=[bankb[Dn[0]]], writes=[d1b[t]])
                    P.op("act", lambda: nc.scalar.activation(out=d1[t][:], in_=d1[t][:], func=AF.Exp, scale=-1.0), reads=[d1b[t]], writes=[d1b[t]])
                    P.op("act", lambda: nc.scalar.activation(out=d2[t][:], in_=banks[Dn[1]][:], func=AF.Ln), reads=[bankb[Dn[1]]], writes=[d2b[t]])
                    P.op("act", lambda: nc.scalar.activation(out=d2[t][:], in_=d2[t][:], func=AF.Exp, scale=-1.0), reads=[d2b[t]], writes=[d2b[t]])
                    P.op("dve", lambda: nc.vector.tensor_tensor(out=o1[t][:], in0=o1[t][:], in1=d1[t][:], op=ALU.mult),
                         reads=[o1b[t], d1b[t]], writes=[o1b[t]])
                    P.op("dve", lambda: nc.vector.tensor_tensor(out=o2[t][:], in0=o2[t][:], in1=d2[t][:], op=ALU.mult),
                         reads=[o2b[t], d2b[t]], writes=[o2b[t]])
                    P.op("dve", lambda: nc.vector.scalar_tensor_tensor(out=o1[t][:], in0=o2[t][:], scalar=lsm[:, 5:6], in1=o1[t][:],
                                                                       op0=ALU.mult, op1=ALU.add),
                         reads=[o2b[t], o1b[t], lsmb], writes=[o1b[t]])
                    P.op("dve", lambda: nc.vector.tensor_tensor(out=sqo[t][:], in0=o1[t][:], in1=o1[t][:], op=ALU.mult),
                         reads=[o1b[t]], writes=[sqob[t]])

                def fin_b(h, i):
                    t = (h * NQ + i) % 2
                    q0 = i * QW
                    P.mm(bankb[MS], banks[MS][:], [(ones32[:], sqo[t][:])], reads=[sqob[t]] + constb)
                    P.op("act", lambda: nc.scalar.activation(out=rst[t][:], in_=banks[MS][:], func=AF.Ln, bias=eps128[:, 0:1], scale=1.0 / 128),
                         reads=[bankb[MS], epsb], writes=[rstb[t]])
                    P.op("act", lambda: nc.scalar.activation(out=rst[t][:], in_=rst[t][:], func=AF.Exp, scale=-0.5), reads=[rstb[t]], writes=[rstb[t]])
                    P.op("pool", lambda: nc.gpsimd.tensor_tensor(out=o1[t][:], in0=o1[t][:], in1=rst[t][:], op=ALU.mult),
                         reads=[o1b[t], rstb[t]], writes=[o1b[t]])
                    P.op("dve", lambda: nc.vector.tensor_scalar(out=ybs[t][:], in0=o1[t][:], scalar1=ang[:, h:h + 1], scalar2=None, op0=ALU.mult),
                         reads=[o1b[t], angb], writes=[ybsb[t]])
                    P.dma("sp", YBD[h * 128:(h + 1) * 128, q0:q0 + QW], ybs[t][:], ybsb[t], reads=[ybsb[t]])

                load_head(0)
                for k in range(min(LOOK, NU)):
                    emit_S(k)
                for k in range(NU):
                    h, i, bi, nblk, kb, c0, kind, s = units[k]
                    hb = h % 2
                    if i == 0 and bi == 0 and s == 0 and h + 1 < 8:
                        load_head(h + 1)
                    if k + LOOK < NU:
                        emit_S(k + LOOK)
                    sb_ = SB[k % NSB]
                    pi = k % 4
                    p_t = pt[pi]
                    P.op("act", lambda: nc.scalar.activation(out=p_t[:, c0:512], in_=banks[sb_][:, c0:512], func=AF.Exp),
                         reads=[bankb[sb_]], writes=[ptb[pi]])
                    P.mm(bankb[O[s]], banks[O[s]][:, c0:512], [(VH[hb][:, kb, :], p_t[:, c0:512])],
                         reads=[VHb[hb], ptb[pi]], start=(bi == 0), stop=(bi == nblk - 1))
                    tt = (h * NQ + i) % 2
                    if bi == 0:
                        P.op("dve", lambda: nc.vector.tensor_copy(out=dacc[s][tt][:], in_=p_t[:]), reads=[ptb[pi]], writes=[daccb[s][tt]])
                    else:
                        P.op("dve", lambda: nc.vector.tensor_tensor(out=dacc[s][tt][:, c0:512], in0=dacc[s][tt][:, c0:512], in1=p_t[:, c0:512], op=ALU.add),
                             reads=[ptb[pi], daccb[s][tt]], writes=[daccb[s][tt]])
                    for ent in pending:
                        ent[0] -= 1
                    while pending and pending[0][0] <= 0:
                        ent = pending.pop(0)
                        fin_b(ent[1], ent[2])
                    if bi == nblk - 1 and s == 1:
                        fin_a(h, i)
                        pending.append([12, h, i])
                while pending:
                    ent = pending.pop(0)
                    fin_b(ent[1], ent[2])
                P.barrier()

        def merge_phase():
            with ExitStack() as es0:
                MG = sbt(es0, "mgMG", [128, 16, OWN], BF16)
                MGb = [[Buf() for _ in range(NQ)] for _ in range(16)]
                with ExitStack() as es:
                    tag = "m1"
                    YAs = sbt(es, tag + "ya", [128, 8, OWN], BF16)
                    YBs = sbt(es, tag + "yb", [128, 8, OWN], BF16)
                    wab = [sbt(es, tag + "wab%d" % i, [128, 2, 8, 128], BF16) for i in range(2)]
                    gat = [sbt(es, tag + "ga%d" % i, [128, 512], F32) for i in range(4)]
                    gbt = [sbt(es, tag + "gb%d" % i, [128, 512], F32) for i in range(4)]
                    t1 = [sbt(es, tag + "t1%d" % i, [128, 512], F32) for i in range(2)]
                    yab, ybb = Buf(), Buf()
                    wabb = [Buf() for _ in range(2)]
                    gatb = [Buf() for _ in range(4)]
                    gbtb = [Buf() for _ in range(4)]
                    t1b = [Buf() for _ in range(2)]
                    P.dma("sp", YAs[:], YA.rearrange("(k p) t -> p k t", p=128), yab, writes=[yab])
                    P.dma("sp", YBs[:], YBD.rearrange("(k p) t -> p k t", p=128), ybb, writes=[ybb])
                    items = [(f, q) for f in range(16) for q in range(NQ)]

                    def load(i):
                        f, q = items[i]
                        qs = slice(q * QW, (q + 1) * QW)
                        P.dma("sp", gat[i % 4][:], GA[f * 128:(f + 1) * 128, qs], gatb[i % 4], writes=[gatb[i % 4]])
                        P.dma("sp", gbt[i % 4][:], GB[f * 128:(f + 1) * 128, qs], gbtb[i % 4], writes=[gbtb[i % 4]])
                    for i in range(3):
                        load(i)
                    for i, (f, q) in enumerate(items):
                        qs = slice(q * QW, (q + 1) * QW)
                        if q == 0:
                            sl = f % 2
                            P.dma("pool", wab[sl][:, 0, :, :], WA[f].rearrange("p (k c) -> p k c", k=8), wabb[sl], writes=[wabb[sl]])
                            P.dma("pool", wab[sl][:, 1, :, :], WB[f].rearrange("p (k c) -> p k c", k=8), wabb[sl], writes=[wabb[sl]])
                        if i + 3 < len(items):
                            load(i + 3)
                        ba = bank()
                        P.mm(bankb[ba], banks[ba][:], [(wab[sl][:, 0, kc, :], YAs[:, kc, qs]) for kc in range(8)], reads=[wabb[sl], yab])
                        bb = bank()
                        P.mm(bankb[bb], banks[bb][:], [(wab[sl][:, 1, kc, :], YBs[:, kc, qs]) for kc in range(8)], reads=[wabb[sl], ybb])
                        P.op("dve", lambda: nc.vector.tensor_tensor(out=gat[i % 4][:], in0=banks[ba][:], in1=gat[i % 4][:], op=ALU.mult),
                             reads=[bankb[ba], gatb[i % 4]], writes=[gatb[i % 4]])
                        P.op("dve", lambda: nc.vector.tensor_tensor(out=gbt[i % 4][:], in0=banks[bb][:], in1=gbt[i % 4][:], op=ALU.mult),
                             reads=[bankb[bb], gbtb[i % 4]], writes=[gbtb[i % 4]])
                        P.op("dve", lambda: nc.vector.tensor_tensor(out=MG[:, f, qs], in0=gat[i % 4][:], in1=gbt[i % 4][:], op=ALU.add),
                             reads=[gatb[i % 4], gbtb[i % 4]], writes=[MGb[f][q]])
                P.barrier()
                with ExitStack() as es:
                    tag = "m2"
                    wo = sbt(es, tag + "wo", [128, 16, 16, 128], BF16)
                    wob = [Buf() for _ in range(16)]
                    sin = [sbt(es, tag + "sin%d" % i, [128, 512], F32) for i in range(4)]
                    sinb = [Buf() for _ in range(4)]
                    lout = [sbt(es, tag + "lout%d" % i, [128, 512], F32) for i in range(3)]
                    loutb = [Buf() for _ in range(3)]
                    for f in range(16):
                        P.dma("pool", wo[:, f, :, :], WO[f].rearrange("p (k c) -> p k c", k=16), wob[f], writes=[wob[f]])

                    def prefetch(i):
                        q, f = divmod(i, 16)
                        P.dma("sp", sin[i % 4][:], X1[f * 128:(f + 1) * 128, q * QW:(q + 1) * QW], sinb[i % 4], writes=[sinb[i % 4]])

                    def produce(q, f, i, ydst, ydb):
                        qs = slice(q * QW, (q + 1) * QW)
                        bk = bank()
                        P.mm(bankb[bk], banks[bk][:], [(wo[:, f, kc, :], MG[:, kc, qs]) for kc in range(16)],
                             reads=[wob[f]] + [MGb[kc][q] for kc in range(16)])
                        si = sin[i % 4]
                        P.op("act", lambda: nc.scalar.mul(out=si[:], in_=si[:], mul=ALPHA), reads=[sinb[i % 4]], writes=[sinb[i % 4]])
                        P.op("dve", lambda: nc.vector.tensor_tensor(out=ydst, in0=banks[bk][:], in1=si[:], op=ALU.add),
                             reads=[bankb[bk], sinb[i % 4]], writes=[ydb])

                    ln_pipeline(es, tag, X2, 0, 1, produce, prefetch, 3, lout, loutb)
                P.barrier()

        def pe_phase():
            with ExitStack() as es:
                tag = "pe"
                xq = [sbt(es, tag + "xq%d" % i, [128, 16, QW], BF16) for i in range(2)]
                pT = sbt(es, tag + "pT", [128, 2, OWN], BF16)
                wg = sbt(es, tag + "wg", [128, 16, 16, 128], BF16)
                wp = sbt(es, tag + "wp", [128, 16, 2, 128], BF16)
                sgt = [sbt(es, tag + "sg%d" % i, [128, 512], F32) for i in range(2)]
                sin = [sbt(es, tag + "sin%d" % i, [128, 512], F32) for i in range(4)]
                lout = [sbt(es, tag + "lout%d" % i, [128, 512], F32) for i in range(3)]
                xqb = [Buf() for _ in range(2)]
                pTb, wpb = Buf(), Buf()
                wgb = [Buf() for _ in range(16)]
                sgb = [Buf() for _ in range(2)]
                sinb = [Buf() for _ in range(4)]
                loutb = [Buf() for _ in range(3)]

                def load_xq(q):
                    P.dma("pool", xq[q % 2][:], X3[:, q * QW:(q + 1) * QW].rearrange("(f p) t -> p f t", p=128),
                          xqb[q % 2], writes=[xqb[q % 2]])

                load_xq(0)
                P.dma("pool", pT[:], PT.rearrange("(k p) t -> p k t", p=128), pTb, writes=[pTb])
                P.dma("pool", wp[:], WP.rearrange("f p (k c) -> p f k c", k=2), wpb, writes=[wpb])
                for f in range(16):
                    P.dma("pool", wg[:, f, :, :], WG[f].rearrange("p (k c) -> p k c", k=16), wgb[f], writes=[wgb[f]])
                    if f == 3:
                        load_xq(1)

                def prefetch(i):
                    q, f = divmod(i, 16)
                    P.dma("sp", sin[i % 4][:], X3[f * 128:(f + 1) * 128, q * QW:(q + 1) * QW], sinb[i % 4], writes=[sinb[i % 4]])

                def produce(q, f, i, ydst, ydb):
                    qs = slice(q * QW, (q + 1) * QW)
                    if f == 0 and q >= 1 and q + 1 < NQ:
                        load_xq(q + 1)
                    xt = xq[q % 2]
                    bg = bank()
                    P.mm(bankb[bg], banks[bg][:], [(wg[:, f, kc, :], xt[:, kc, :]) for kc in range(16)], reads=[wgb[f], xqb[q % 2]])
                    bp = bank()
                    P.mm(bankb[bp], banks[bp][:], [(wp[:, f, kc, :], pT[:, kc, qs]) for kc in range(2)], reads=[wpb, pTb])
                    k2 = i % 2
                    si = sin[i % 4]
                    P.op("act", lambda: nc.scalar.activation(out=sgt[k2][:], in_=banks[bg][:], func=AF.Sigmoid), reads=[bankb[bg]], writes=[sgb[k2]])
                    P.op("dve", lambda: nc.vector.tensor_tensor(out=sgt[k2][:], in0=banks[bp][:], in1=sgt[k2][:], op=ALU.mult),
                         reads=[bankb[bp], sgb[k2]], writes=[sgb[k2]])
                    P.op("dve", lambda: nc.vector.scalar_tensor_tensor(out=ydst, in0=si[:], scalar=ALPHA, in1=sgt[k2][:],
                                                                       op0=ALU.mult, op1=ALU.add),
                         reads=[sgb[k2], sinb[i % 4]], writes=[ydb])

                ln_pipeline(es, tag, OUT, 0, 3, produce, prefetch, 3, lout, loutb)
                P.barrier()

        ffn_phase(X0, X1, 0, WGU[0], WDN[0], 0, "f1a")
        if STAGE >= 2:
            ffn_phase(X0, X1, OWN, WGU[0], WDN[0], 0, "f1b")
        if STAGE >= 3:
            proj_phase()
        if STAGE >= 4:
            attn_phase()
        if STAGE >= 5:
            merge_phase()
        if STAGE >= 6:
            ffn_phase(X2, X3, 0, WGU[1], WDN[1], 2, "f2")
        if STAGE >= 7:
            pe_phase()

        P.barrier()
    return nc


def _prep_shared(inp):
    sh = {}
    ffw = {1: (inp["ffn1_w_gu"], inp["ffn1_w_down"]), 2: (inp["ffn2_w_gu"], inp["ffn2_w_down"])}
    for i in (1, 2):
        wgu = np.asarray(ffw[i][0][0], np.float32)
        t = wgu.reshape(16, 128, 2, NJ, 128)
        sh["wgu%d" % i] = np.ascontiguousarray(t.transpose(3, 1, 2, 0, 4)).reshape(NJ, 128, 2 * 16 * 128)
        wd = np.asarray(ffw[i][1][0], np.float32)
        t = wd.reshape(NPART, JP, 128, 8, 2, 128)
        sh["wdn%d" % i] = np.ascontiguousarray(t.transpose(0, 3, 2, 4, 1, 5)).reshape(NPART, 8, 128, 2 * JP * 128)
    win = np.asarray(inp["w_in"][0], np.float32)
    t = win.reshape(16, 128, 72, 128)
    sh["win"] = np.ascontiguousarray(t.transpose(2, 1, 0, 3)).reshape(72, 128, 16 * 128)
    rows = []
    for c0 in (1024, 4096):
        blk = win[:, c0:c0 + 1024].reshape(16, 128, 1024)
        rows.append(np.ascontiguousarray(blk.transpose(1, 0, 2)).reshape(128, 16 * 1024))
    sh["wrow"] = np.stack(rows)

    def ftile(w, nk):
        t = np.asarray(w, np.float32).reshape(nk, 128, 16, 128)
        return np.ascontiguousarray(t.transpose(2, 1, 0, 3)).reshape(16, 128, nk * 128)

    sh["wa"] = ftile(inp["w_branch_a"][0], 8)
    sh["wb"] = ftile(inp["w_branch_b"][0], 8)
    sh["wo"] = ftile(inp["w_out"][0], 16)
    sh["wg"] = ftile(inp["w_pe_gate"][0], 16)
    sh["wp"] = ftile(inp["w_pe_proj"][0], 2)
    lgs = [inp["ln1_g"], inp["ln2_g"], inp["ln3_g"], inp["ln4_g"]]
    lbs = [inp["ln1_b"], inp["ln2_b"], inp["ln3_b"], inp["ln4_b"]]
    sh["lng"] = np.stack([np.ascontiguousarray(np.asarray(a[0], np.float32).reshape(16, 128).T) for a in lgs])
    sh["lnb"] = np.stack([np.ascontiguousarray(np.asarray(a[0], np.float32).reshape(16, 128).T) for a in lbs])
    sh["sgug"] = np.asarray(inp["sgu_ln_g"], np.float32).reshape(1, 1024)
    sh["sgub"] = np.asarray(inp["sgu_ln_b"], np.float32).reshape(1, 1024)
    sw = np.asarray(inp["sgu_w"][0], np.float32)
    sh["sguwt"] = np.ascontiguousarray(sw.transpose(2, 0, 1)).reshape(128, 8 * 128)
    pos = np.arange(128)
    sh["sgumask"] = ((pos[:, None] // 64) <= (pos[None, :] // 64)).astype(np.float32)
    sh["sgubias"] = np.asarray(inp["sgu_b"][0], np.float32).reshape(1, 8 * 128)
    sh["lamv"] = np.stack([np.asarray(a[0], np.float32) for a in (inp["lam_q1"], inp["lam_k1"], inp["lam_q2"], inp["lam_k2"])])
    sh["ang"] = np.ascontiguousarray(np.asarray(inp["attn_norm_g"][0], np.float32).reshape(8, 128).T)
    slopes = 2.0 ** (-np.arange(1, 9, dtype=np.float64))
    kl = pos[:, None]
    ql = pos[None, :]
    allowed = (kl // 64) <= (ql // 64)
    diag = np.zeros((8, 128, 128), np.float32)
    for hh in range(8):
        corr = -2.0 * slopes[hh] * np.maximum(kl - ql, 0)
        diag[hh] = np.where(allowed, corr, NEG)
    sh["diag"] = diag.astype(ml_dtypes.bfloat16)
    sh["ident"] = np.eye(128, dtype=np.float32).astype(ml_dtypes.bfloat16)
    return sh, slopes


def _prep_core(inp, sh, slopes, b, r):
    m = dict(sh)
    x = np.asarray(inp["x"][b], np.float32).reshape(32, 128, D)
    own = x[r::2].reshape(OWN, D)
    oth = x[(1 - r)::2].reshape(OWN, D)
    m["x0"] = np.ascontiguousarray(np.concatenate([own, oth], 0).T)
    p = np.asarray(inp["p"][0, b], np.float32).reshape(32, 128, 256)[r::2].reshape(OWN, 256)
    m["pt"] = np.ascontiguousarray(p.T)
    blk = np.arange(16)
    gpos_own = ((2 * blk + r)[:, None] * 128 + np.arange(128)[None, :]).reshape(-1)
    gpos_oth = ((2 * blk + (1 - r))[:, None] * 128 + np.arange(128)[None, :]).reshape(-1)
    gk = np.concatenate([gpos_own, gpos_oth])
    posk = np.zeros((8, 4, ALL), np.float32)
    posq = np.zeros((8, 4, OWN), np.float32)
    for hh in range(8):
        s = slopes[hh]
        posk[hh, 0] = 1.0
        posk[hh, 1] = 1.0
        posk[hh, 2] = s * 128.0 * (gk // 128)
        posk[hh, 3] = s * (gk % 128)
        posq[hh, 0] = -s * 128.0 * (gpos_own // 128)
        posq[hh, 1] = -s * (gpos_own % 128)
        posq[hh, 2] = 1.0
        posq[hh, 3] = 1.0
    m["posk"] = posk.astype(ml_dtypes.bfloat16)
    m["posq"] = posq.astype(ml_dtypes.bfloat16)
    m["pmask"] = np.full((128, 128), NEG if r == 0 else 0.0, np.float32).astype(ml_dtypes.bfloat16)
    return m


_NC_CACHE = {}


def kernel(**inputs):
    sh, slopes = _prep_shared(inputs)
    in_maps = []
    for c in range(8):
        in_maps.append(_prep_core(inputs, sh, slopes, c // 2, c % 2))
    if "nc" not in _NC_CACHE:
        _NC_CACHE["nc"] = build_program()
    nc = _NC_CACHE["nc"]
    res = run_bass_kernel_spmd(nc, in_maps, core_ids=list(range(8)))
    if STAGE < 99:
        return res
    out = np.zeros((NBATCH, 32, 128, D), np.float32)
    for c in range(8):
        b, r = c // 2, c % 2
        o = np.asarray(res.results[c]["out"], np.float32)
        out[b, r::2] = o.T.reshape(16, 128, D)
    return out.reshape(NBATCH, SEQ, D)
```

```python
import os
import math
from contextlib import ExitStack

import numpy as np
import ml_dtypes

import concourse.bass as bass
import concourse.mybir as mybir
from concourse.bass_utils import run_bass_kernel_spmd

F32 = mybir.dt.float32
BF16 = mybir.dt.bfloat16
AF = mybir.ActivationFunctionType
ALU = mybir.AluOpType

D = 2048
SEQ = 4096
NBATCH = 4
DFF = 5632
NJ = 44
NPART = 4
JP = 11
OWN = 2048
ALL = 4096
NQ = 4
QW = 512
ALPHA = float(2.0 ** 0.25)
LN_EPS = 1e-5
LAM_INIT = 0.8 - 0.6 * math.exp(0.0)
NEG = -30000.0
STAGE = int(os.environ.get("MK_STAGE", "99"))


class Buf:
    __slots__ = ("w", "r", "ds")

    def __init__(self):
        self.w = None
        self.r = {}
        self.ds = None


class DSem:
    __slots__ = ("h", "cnt", "key", "last")

    def __init__(self, h, key):
        self.h = h
        self.cnt = 0
        self.key = key
        self.last = None


class Prog:
    def __init__(self, nc):
        self.nc = nc
        self.eng = {"pe": nc.tensor, "act": nc.scalar, "dve": nc.vector, "pool": nc.gpsimd, "sp": nc.sync}
        self.sem = {k: nc.alloc_semaphore("s_" + k) for k in self.eng}
        self.cnt = {k: 0 for k in self.eng}
        self.seen = {k: {} for k in self.eng}
        self.free_ds = {"sw": [DSem(nc.alloc_semaphore("w%d" % i), "w%d" % i) for i in range(26)],
                        "hw": [DSem(nc.alloc_semaphore("d%d" % i), "d%d" % i) for i in range(46)]}
        self.used_ds = []
        self.nbank = 0

    def wait(self, eng, ev):
        key, h, v = ev
        if eng == "pe" and key == "pe":
            return
        if self.seen[eng].get(key, 0) >= v:
            return
        self.eng[eng].wait_ge(h, v)
        self.seen[eng][key] = v

    def _deps(self, eng, reads, writes):
        for b in reads:
            if b.w is not None:
                self.wait(eng, b.w)
        for b in writes:
            if b.w is not None:
                self.wait(eng, b.w)
            for ev in b.r.values():
                self.wait(eng, ev)

    def _mark(self, ev, reads, writes):
        for b in reads:
            b.r[ev[0]] = ev
        for b in writes:
            b.w = ev
            b.r = {}

    def op(self, eng, fn, reads=(), writes=()):
        self._deps(eng, reads, writes)
        ins = fn()
        self.cnt[eng] += 1
        ins.then_inc(self.sem[eng], 1)
        ev = (eng, self.sem[eng], self.cnt[eng])
        self._mark(ev, reads, writes)
        return ev

    def dma(self, q, out_ap, in_ap, owner, reads=(), writes=()):
        self._deps(q, reads, writes)
        kind = "sw" if q == "pool" else "hw"
        if owner.ds is None:
            owner.ds = self.free_ds[kind].pop()
            self.used_ds.append((owner, owner.ds, kind))
        ds = owner.ds
        if ds.last is not None:
            self.wait(q, ds.last)
        ins = self.eng[q].dma_start(out=out_ap, in_=in_ap)
        ds.cnt += 16
        ins.then_inc(ds.h, 16)
        ev = (ds.key, ds.h, ds.cnt)
        ds.last = ev
        self._mark(ev, reads, writes)
        return ev

    def barrier(self):
        for e in self.eng:
            for k in self.eng:
                if k != e and self.cnt[k] > 0:
                    self.wait(e, (k, self.sem[k], self.cnt[k]))
        for owner, ds, kind in self.used_ds:
            if ds.last is not None:
                for e in self.eng:
                    self.wait(e, ds.last)
            owner.ds = None
            self.free_ds[kind].append(ds)
        self.used_ds = []

    def mm(self, bankbuf, out_ap, pairs, reads, start=True, stop=True):
        nc = self.nc

        def fn():
            n = len(pairs)
            ins = None
            for i, (l, r) in enumerate(pairs):
                ins = nc.tensor.matmul(out_ap, lhsT=l, rhs=r, start=(start and i == 0), stop=(stop and i == n - 1))
            return ins

        return self.op("pe", fn, reads=reads, writes=[bankbuf])


def build_program():
    nc = bass.Bass("TRN2", target_bir_lowering=False)
    P = Prog(nc)

    def din(name, shape, dt=F32):
        return nc.dram_tensor(name, list(shape), dt, kind="ExternalInput").ap()

    def dscr(name, shape, dt=F32, out=False):
        return nc.dram_tensor(name, list(shape), dt, kind=("ExternalOutput" if out else "Internal")).ap()

    X0 = din("x0", [D, ALL])
    PT = din("pt", [256, OWN])
    WGU = [din("wgu%d" % i, [NJ, 128, 2 * 16 * 128]) for i in (1, 2)]
    WDN = [din("wdn%d" % i, [NPART, 8, 128, 2 * JP * 128]) for i in (1, 2)]
    WIN = din("win", [72, 128, 16 * 128])
    WROW = din("wrow", [2, 128, 16 * 1024])
    WA = din("wa", [16, 128, 8 * 128])
    WB = din("wb", [16, 128, 8 * 128])
    WO = din("wo", [16, 128, 16 * 128])
    WG = din("wg", [16, 128, 16 * 128])
    WP = din("wp", [16, 128, 2 * 128])
    LNG = din("lng", [4, 128, 16])
    LNB = din("lnb", [4, 128, 16])
    SGUG = din("sgug", [1, 1024])
    SGUB = din("sgub", [1, 1024])
    SGUWT = din("sguwt", [128, 8 * 128])
    SGUMASK = din("sgumask", [128, 128])
    SGUBIAS = din("sgubias", [1, 8 * 128])
    LAMV = din("lamv", [4, 64])
    ANG = din("ang", [128, 8])
    POSK = din("posk", [8, 4, ALL], BF16)
    POSQ = din("posq", [8, 4, OWN], BF16)
    DIAG = din("diag", [8, 128, 128], BF16)
    IDENT = din("ident", [128, 128], BF16)
    PMASK = din("pmask", [128, 128], BF16)

    dbg = STAGE < 99
    X1 = dscr("x1", [D, ALL], out=dbg)
    ACC = dscr("acc", [D, OWN])
    YS = dscr("ys", [D, OWN])
    QT = dscr("qt", [1024, OWN], BF16, out=dbg)
    KT = dscr("kt", [1024, ALL], BF16, out=dbg)
    VAL = dscr("val", [ALL, 1024], BF16, out=dbg)
    YA = dscr("ya", [1024, OWN], BF16, out=dbg)
    GA = dscr("ga", [D, OWN])
    GB = dscr("gb", [D, OWN])
    X2 = dscr("x2", [D, OWN], out=dbg)
    X3 = dscr("x3", [D, OWN], out=dbg)
    YBD = dscr("ybd", [1024, OWN], BF16, out=dbg)
    OUT = dscr("out", [D, OWN], out=True)

    top = ExitStack()
    with top:
        def sbt(es, name, shape, dt):
            return es.enter_context(nc.sbuf_tensor(name, list(shape), dt))

        banks = [top.enter_context(nc.psum_tensor("ps%d" % i, [128, 512], F32)) for i in range(8)]
        bankb = [Buf() for _ in range(8)]

        def bank():
            i = P.nbank % 8
            P.nbank += 1
            return i

        ones32 = sbt(top, "ones32", [128, 128], F32)
        onesb = sbt(top, "onesb", [128, 128], BF16)
        lng = sbt(top, "lng_s", [128, 4, 16], F32)
        lnb = sbt(top, "lnb_s", [128, 4, 16], F32)
        epsc = sbt(top, "epsc", [128, 1], F32)
        cb = Buf()
        P.op("dve", lambda: nc.vector.memset(ones32[:], 1.0), writes=[cb])
        P.op("dve", lambda: nc.vector.memset(onesb[:], 1.0), writes=[cb])
        P.op("dve", lambda: nc.vector.memset(epsc[:], LN_EPS), writes=[cb])
        lb1, lb2 = Buf(), Buf()
        P.dma("sp", lng[:], LNG.rearrange("l p f -> p l f"), lb1, writes=[lb1])
        P.dma("sp", lnb[:], LNB.rearrange("l p f -> p l f"), lb2, writes=[lb2])
        constb = [cb, lb1, lb2]

        def ln_pass(es, Ysrc, Xdst, tok0, s1, s2, s1b, s2b, li, tag, lin, linb, lout, loutb):
            mean = sbt(es, tag + "mean", [128, 512], F32)
            rstd = sbt(es, tag + "rstd", [128, 512], F32)
            nmr = sbt(es, tag + "nmr", [128, 512], F32)
            mb, rb, nb = Buf(), Buf(), Buf()
            for q in range(NQ):
                qs = slice(q * QW, (q + 1) * QW)
                b1 = bank()
                P.mm(bankb[b1], banks[b1][:], [(ones32[:], s1[:, qs])], reads=[s1b[q]] + constb)
                b2 = bank()
                P.mm(bankb[b2], banks[b2][:], [(ones32[:], s2[:, qs])], reads=[s2b[q]] + constb)
                P.op("dve", lambda: nc.vector.tensor_scalar(out=mean[:], in0=banks[b1][:], scalar1=1.0 / D, scalar2=None, op0=ALU.mult),
                     reads=[bankb[b1]], writes=[mb])
                P.op("dve", lambda: nc.vector.tensor_tensor(out=nmr[:], in0=mean[:], in1=mean[:], op=ALU.mult), reads=[mb], writes=[nb])
                P.op("dve", lambda: nc.vector.scalar_tensor_tensor(out=rstd[:], in0=banks[b2][:], scalar=1.0 / D, in1=nmr[:],
                                                                   op0=ALU.mult, op1=ALU.subtract), reads=[bankb[b2], nb], writes=[rb])
                P.op("act", lambda: nc.scalar.activation(out=rstd[:], in_=rstd[:], func=AF.Sqrt, bias=epsc[:, 0:1], scale=1.0),
                     reads=[rb] + constb, writes=[rb])
                P.op("dve", lambda: nc.vector.reciprocal(out=rstd[:], in_=rstd[:]), reads=[rb], writes=[rb])
                P.op("dve", lambda: nc.vector.scalar_tensor_tensor(out=nmr[:], in0=mean[:], scalar=-1.0, in1=rstd[:],
                                                                   op0=ALU.mult, op1=ALU.mult), reads=[mb, rb], writes=[nb])

                def load(f):
                    P.dma("sp", lin[f % 4][:], Ysrc[f * 128:(f + 1) * 128, qs], linb[f % 4],
                          reads=[ysb[(f, q)]], writes=[linb[f % 4]])

                for f in range(3):
                    load(f)
                for f in range(16):
                    if f + 3 < 16:
                        load(f + 3)
                    li_t, lo_t = lin[f % 4], lout[f % 3]
                    P.op("dve", lambda: nc.vector.tensor_tensor(out=li_t[:], in0=li_t[:], in1=rstd[:], op=ALU.mult),
                         reads=[linb[f % 4], rb], writes=[linb[f % 4]])
                    P.op("dve", lambda: nc.vector.tensor_tensor(out=li_t[:], in0=li_t[:], in1=nmr[:], op=ALU.add),
                         reads=[linb[f % 4], nb], writes=[linb[f % 4]])
                    P.op("act", lambda: nc.scalar.activation(out=lo_t[:], in_=li_t[:], func=AF.Identity,
                                                             bias=lnb[:, li, f:f + 1], scale=lng[:, li, f:f + 1]),
                         reads=[linb[f % 4]] + constb, writes=[loutb[f % 3]])
                    P.dma("act", Xdst[f * 128:(f + 1) * 128, tok0 + q * QW: tok0 + (q + 1) * QW], lo_t[:], loutb[f % 3],
                          reads=[loutb[f % 3]])

        ysb = {(f, q): Buf() for f in range(16) for q in range(NQ)}
        accb = {(f, q): Buf() for f in range(16) for q in range(NQ)}

        def stats_update(y_t, yb, s1, s2, s1b, s2b, q, sq_t, sqb, first):
            qs = slice(q * QW, (q + 1) * QW)
            if first:
                P.op("dve", lambda: nc.vector.tensor_copy(out=s1[:, qs], in_=y_t[:]), reads=[yb], writes=[s1b[q]])
                P.op("act", lambda: nc.scalar.activation(out=s2[:, qs], in_=y_t[:], func=AF.Square), reads=[yb], writes=[s2b[q]])
            else:
                P.op("dve", lambda: nc.vector.tensor_tensor(out=s1[:, qs], in0=s1[:, qs], in1=y_t[:], op=ALU.add),
                     reads=[yb, s1b[q]], writes=[s1b[q]])
                P.op("act", lambda: nc.scalar.activation(out=sq_t[:], in_=y_t[:], func=AF.Square), reads=[yb], writes=[sqb])
                P.op("dve", lambda: nc.vector.tensor_tensor(out=s2[:, qs], in0=s2[:, qs], in1=sq_t[:], op=ALU.add),
                     reads=[sqb, s2b[q]], writes=[s2b[q]])

        def ln_pipeline(es, tag, Xout, tok0, li, produce, prefetch, depth, lout, loutb):
            ybuf = sbt(es, tag + "ybuf", [128, 16, 512], F32)
            ybb = [Buf() for _ in range(16)]
            s1 = sbt(es, tag + "ls1", [128, 512], F32)
            s2 = sbt(es, tag + "ls2", [128, 512], F32)
            s1b, s2b = Buf(), Buf()
            sqt = [sbt(es, tag + "lsq%d" % i, [128, 512], F32) for i in range(2)]
            sqb = [Buf() for _ in range(2)]
            mean = sbt(es, tag + "lmean", [128, 512], F32)
            mb = Buf()
            rstd = [sbt(es, tag + "lrstd%d" % i, [128, 512], F32) for i in range(2)]
            nmr = [sbt(es, tag + "lnmr%d" % i, [128, 512], F32) for i in range(2)]
            rb = [Buf() for _ in range(2)]
            nb = [Buf() for _ in range(2)]
            nlo = len(lout)

            def stats_fin(q):
                par = q % 2
                flush_s2()
                b1 = bank()
                P.mm(bankb[b1], banks[b1][:], [(ones32[:], s1[:])], reads=[s1b] + constb)
                b2 = bank()
                P.mm(bankb[b2], banks[b2][:], [(ones32[:], s2[:])], reads=[s2b] + constb)
                P.op("dve", lambda: nc.vector.tensor_scalar(out=mean[:], in0=banks[b1][:], scalar1=1.0 / D, scalar2=None, op0=ALU.mult),
                     reads=[bankb[b1]], writes=[mb])
                P.op("dve", lambda: nc.vector.tensor_tensor(out=nmr[par][:], in0=mean[:], in1=mean[:], op=ALU.mult), reads=[mb], writes=[nb[par]])
                P.op("dve", lambda: nc.vector.scalar_tensor_tensor(out=rstd[par][:], in0=banks[b2][:], scalar=1.0 / D, in1=nmr[par][:],
                                                                   op0=ALU.mult, op1=ALU.subtract), reads=[bankb[b2], nb[par]], writes=[rb[par]])
                P.op("act", lambda: nc.scalar.activation(out=rstd[par][:], in_=rstd[par][:], func=AF.Ln, bias=epsc[:, 0:1], scale=1.0),
                     reads=[rb[par]] + constb, writes=[rb[par]])
                P.op("act", lambda: nc.scalar.activation(out=rstd[par][:], in_=rstd[par][:], func=AF.Exp, scale=-0.5),
                     reads=[rb[par]], writes=[rb[par]])
                P.op("dve", lambda: nc.vector.scalar_tensor_tensor(out=nmr[par][:], in0=mean[:], scalar=-1.0, in1=rstd[par][:],
                                                                   op0=ALU.mult, op1=ALU.mult), reads=[mb, rb[par]], writes=[nb[par]])

            def ln_tile(q, f, meng="pool", tail=False):
                par = q % 2
                k = (q * 16 + f) % nlo
                y = ybuf[:, f, :]
                me = nc.gpsimd if meng == "pool" else nc.vector
                P.op(meng, lambda: me.tensor_tensor(out=y, in0=y, in1=rstd[par][:], op=ALU.mult),
                     reads=[ybb[f], rb[par]], writes=[ybb[f]])
                aeng = "pool" if (meng == "pool" and f % 2 == 1 and not tail) else "dve"
                ae = nc.gpsimd if aeng == "pool" else nc.vector
                P.op(aeng, lambda: ae.tensor_tensor(out=y, in0=y, in1=nmr[par][:], op=ALU.add),
                     reads=[ybb[f], nb[par]], writes=[ybb[f]])
                P.op("act", lambda: nc.scalar.activation(out=lout[k][:], in_=y, func=AF.Identity,
                                                         bias=lnb[:, li, f:f + 1], scale=lng[:, li, f:f + 1]),
                     reads=[ybb[f]] + constb, writes=[loutb[k]])
                P.dma("act", Xout[f * 128:(f + 1) * 128, tok0 + q * QW: tok0 + (q + 1) * QW], lout[k][:], loutb[k], reads=[loutb[k]])

            pend = []

            def flush_s2():
                while pend:
                    f0 = pend.pop(0)
                    P.op("dve", lambda: nc.vector.tensor_tensor(out=s2[:], in0=s2[:], in1=sqt[f0 % 2][:], op=ALU.add),
                         reads=[sqb[f0 % 2], s2b], writes=[s2b])

            def stats_upd(f):
                y = ybuf[:, f, :]
                if f == 0:
                    P.op("dve", lambda: nc.vector.tensor_copy(out=s1[:], in_=y), reads=[ybb[f]], writes=[s1b])
                    P.op("act", lambda: nc.scalar.activation(out=s2[:], in_=y, func=AF.Square), reads=[ybb[f]], writes=[s2b])
                else:
                    P.op("dve", lambda: nc.vector.tensor_tensor(out=s1[:], in0=s1[:], in1=y, op=ALU.add), reads=[ybb[f], s1b], writes=[s1b])
                    P.op("act", lambda: nc.scalar.activation(out=sqt[f % 2][:], in_=y, func=AF.Square), reads=[ybb[f]], writes=[sqb[f % 2]])
                    flush_s2()
                    pend.append(f)

            for i0 in range(depth):
                prefetch(i0)
            for q in range(NQ):
                for f in range(16):
                    i = q * 16 + f
                    if i + depth < NQ * 16:
                        prefetch(i + depth)
                    if q > 0:
                        if f == 0:
                            ln_tile(q - 1, 0)
                            ln_tile(q - 1, 1)
                        if f + 2 < 16:
                            ln_tile(q - 1, f + 2)
                    produce(q, f, i, ybuf[:, f, :], ybb[f])
                    stats_upd(f)
                stats_fin(q)
            for f in range(16):
                ln_tile(NQ - 1, f, "dve" if f % 2 == 0 else "pool", tail=True)

        def ffn_phase(Xin, Xout, tok0, Wgu_d, Wd_d, li, tag):
            with ExitStack() as es:
                xT = sbt(es, tag + "xT", [128, 16, OWN], BF16)
                h = sbt(es, tag + "h", [128, JP, OWN], BF16)
                wgu = [sbt(es, tag + "wgu%d" % i, [128, 2, 16, 128], BF16) for i in range(2)]
                wd = [sbt(es, tag + "wd%d" % i, [128, 2, JP, 128], BF16) for i in range(2)]
                sg = [sbt(es, tag + "sg%d" % i, [128, 512], F32) for i in range(2)]
                sin = [sbt(es, tag + "sin%d" % i, [128, 512], F32) for i in range(4)]
                sout = [sbt(es, tag + "sout%d" % i, [128, 512], F32) for i in range(3)]
                xTb = [Buf() for _ in range(NQ)]
                hb = [[Buf() for _ in range(NQ)] for _ in range(JP)]
                wgub = [Buf() for _ in range(2)]
                wdb = [Buf() for _ in range(2)]
                sgb = [Buf() for _ in range(2)]
                sinb = [Buf() for _ in range(4)]
                soutb = [Buf() for _ in range(3)]
                def load_xq(q):
                    P.dma("pool", xT[:, :, q * QW:(q + 1) * QW],
                          Xin[:, tok0 + q * QW: tok0 + (q + 1) * QW].rearrange("(f p) t -> p f t", p=128),
                          xTb[q], writes=[xTb[q]])
                load_xq(0)
                ngu = 0
                nwd = [0]
                nsg = 0
                for p in range(NPART):
                    if p == NPART - 1:
                        for s_ in range(2):
                            P.dma("pool", wd[s_][:], Wd_d[p, s_].rearrange("p (f j c) -> p f j c", f=2, j=JP), wdb[s_], writes=[wdb[s_]])
                    for jj in range(JP):
                        j = p * JP + jj
                        sl = ngu % 2
                        ngu += 1
                        P.dma("pool", wgu[sl][:], Wgu_d[j].rearrange("p (s k c) -> p s k c", s=2, k=16), wgub[sl], writes=[wgub[sl]])
                        if j == 0:
                            for q in range(1, NQ):
                                load_xq(q)
                        for q in range(NQ):
                            qs = slice(q * QW, (q + 1) * QW)
                            bg = bank()
                            P.mm(bankb[bg], banks[bg][:], [(wgu[sl][:, 0, kc, :], xT[:, kc, qs]) for kc in range(16)],
                                 reads=[wgub[sl], xTb[q]])
                            bu = bank()
                            P.mm(bankb[bu], banks[bu][:], [(wgu[sl][:, 1, kc, :], xT[:, kc, qs]) for kc in range(16)],
                                 reads=[wgub[sl], xTb[q]])
                            st = nsg % 2
                            nsg += 1
                            P.op("act", lambda: nc.scalar.activation(out=sg[st][:], in_=banks[bg][:], func=AF.Silu),
                                 reads=[bankb[bg]], writes=[sgb[st]])
                            P.op("dve", lambda: nc.vector.tensor_tensor(out=h[:, jj, qs], in0=banks[bu][:], in1=sg[st][:], op=ALU.mult),
                                 reads=[bankb[bu], sgb[st]], writes=[hb[jj][q]])
                    if p == NPART - 1:
                        break
                    items = [(f, q) for f in range(16) for q in range(NQ)]

                    def load(i):
                        f, q = items[i]
                        src = Xin if p == 0 else ACC
                        t0 = tok0 if p == 0 else 0
                        rd = [] if p == 0 else [accb[(f, q)]]
                        P.dma("sp", sin[i % 4][:], src[f * 128:(f + 1) * 128, t0 + q * QW: t0 + (q + 1) * QW], sinb[i % 4],
                              reads=rd, writes=[sinb[i % 4]])

                    for i in range(3):
                        load(i)
                    for i, (f, q) in enumerate(items):
                        qs = slice(q * QW, (q + 1) * QW)
                        fg, fi = f // 2, f % 2
                        if fi == 0 and q == 0:
                            sl = nwd[0] % 2
                            nwd[0] += 1
                            P.dma("pool", wd[sl][:], Wd_d[p, fg].rearrange("p (f j c) -> p f j c", f=2, j=JP), wdb[sl], writes=[wdb[sl]])
                        if i + 3 < len(items):
                            load(i + 3)
                        bk = bank()
                        P.mm(bankb[bk], banks[bk][:], [(wd[sl][:, fi, jj, :], h[:, jj, qs]) for jj in range(JP)],
                             reads=[wdb[sl]] + [hb[jj][q] for jj in range(JP)])
                        si, so = sin[i % 4], sout[i % 3]
                        if p == 0:
                            P.op("act", lambda: nc.scalar.mul(out=si[:], in_=si[:], mul=ALPHA), reads=[sinb[i % 4]], writes=[sinb[i % 4]])
                        P.op("dve", lambda: nc.vector.scalar_tensor_tensor(out=so[:], in0=banks[bk][:], scalar=0.5, in1=si[:],
                                                                           op0=ALU.mult, op1=ALU.add),
                             reads=[bankb[bk], sinb[i % 4]], writes=[soutb[i % 3]])
                        P.dma("sp", ACC[f * 128:(f + 1) * 128, qs], so[:], soutb[i % 3], reads=[soutb[i % 3]], writes=[accb[(f, q)]])

                pl = NPART - 1
                wdl = xT[:, 0:JP, :].rearrange("p a b -> p (a b)").rearrange("p (f j c) -> p f j c", f=16, j=JP)
                wdlb = [Buf() for _ in range(8)]
                for fg in range(2, 8):
                    P.dma("pool", wdl[:, 2 * fg:2 * fg + 2, :, :], Wd_d[pl, fg].rearrange("p (f j c) -> p f j c", f=2, j=JP),
                          wdlb[fg], writes=[wdlb[fg]] + (xTb if fg == 2 else []))

                def prefetch(i):
                    q, f = divmod(i, 16)
                    P.dma("sp", sin[i % 4][:], ACC[f * 128:(f + 1) * 128, q * QW:(q + 1) * QW], sinb[i % 4],
                          reads=[accb[(f, q)]], writes=[sinb[i % 4]])

                def produce(q, f, i, ydst, ydb):
                    qs = slice(q * QW, (q + 1) * QW)
                    bk = bank()
                    if f < 4:
                        P.mm(bankb[bk], banks[bk][:], [(wd[f // 2][:, f % 2, jj, :], h[:, jj, qs]) for jj in range(JP)],
                             reads=[wdb[f // 2]] + [hb[jj][q] for jj in range(JP)])
                    else:
                        P.mm(bankb[bk], banks[bk][:], [(wdl[:, f, jj, :], h[:, jj, qs]) for jj in range(JP)],
                             reads=[wdlb[f // 2]] + [hb[jj][q] for jj in range(JP)])
                    P.op("dve", lambda: nc.vector.scalar_tensor_tensor(out=ydst, in0=banks[bk][:], scalar=0.5, in1=sin[i % 4][:],
                                                                       op0=ALU.mult, op1=ALU.add),
                         reads=[bankb[bk], sinb[i % 4]], writes=[ydb])

                ln_pipeline(es, tag, Xout, tok0, li, produce, prefetch, 3, sout, soutb)
                P.barrier()

        def resid_tile(i, f, q, src_ap, src_reads, sin, sinb, sout, soutb, sqt, sqb, s1, s2, s1b, s2b):
            qs = slice(q * QW, (q + 1) * QW)
            si, so = sin[i % 4], sout[i % 3]
            P.op("act", lambda: nc.scalar.mul(out=si[:], in_=si[:], mul=ALPHA), reads=[sinb[i % 4]], writes=[sinb[i % 4]])
            P.op("dve", lambda: nc.vector.tensor_tensor(out=so[:], in0=src_ap, in1=si[:], op=ALU.add),
                 reads=list(src_reads) + [sinb[i % 4]], writes=[soutb[i % 3]])
            P.dma("sp", YS[f * 128:(f + 1) * 128, qs], so[:], soutb[i % 3], reads=[soutb[i % 3]], writes=[ysb[(f, q)]])
            stats_update(so, soutb[i % 3], s1, s2, s1b, s2b, q, sqt[i % 2], sqb[i % 2], f == 0)

        def epi_bufs(es, tag):
            sin = [sbt(es, tag + "sin%d" % i, [128, 512], F32) for i in range(4)]
            sout = [sbt(es, tag + "sout%d" % i, [128, 512], F32) for i in range(3)]
            sqt = [sbt(es, tag + "sq%d" % i, [128, 512], F32) for i in range(2)]
            s1 = sbt(es, tag + "s1", [128, OWN], F32)
            s2 = sbt(es, tag + "s2", [128, OWN], F32)
            return (sin, [Buf() for _ in range(4)], sout, [Buf() for _ in range(3)], sqt, [Buf() for _ in range(2)],
                    s1, s2, [Buf() for _ in range(NQ)], [Buf() for _ in range(NQ)])

        def proj_phase():
            with ExitStack() as es:
                tag = "pj"
                xT = sbt(es, tag + "xT", [128, 16, OWN], BF16)
                wrow = sbt(es, tag + "wrow", [128, 16, 1024], BF16)
                VN = sbt(es, tag + "VN", [128, 16, 1024], BF16)
                wch = [sbt(es, tag + "wch%d" % i, [128, 16, 128], BF16) for i in range(3)]
                gbc = sbt(es, tag + "gbc", [128, 1024], F32)
                bbc = sbt(es, tag + "bbc", [128, 1024], F32)
                wst = sbt(es, tag + "wst", [128, 8, 128], F32)
                wsb = sbt(es, tag + "wsb", [128, 8, 128], BF16)
                msk = sbt(es, tag + "msk", [128, 128], F32)
                sbias = sbt(es, tag + "sbias", [128, 1024], F32)
                vt = [sbt(es, tag + "vt%d" % i, [128, 1024], F32) for i in range(2)]
                vsq = sbt(es, tag + "vsq", [128, 1024], F32)
                sm = sbt(es, tag + "sm", [128, 8], F32)
                ut = [sbt(es, tag + "ut%d" % i, [128, 512], F32) for i in range(2)]
                stmp = [sbt(es, tag + "stmp%d" % i, [128, 512], F32) for i in range(2)]
                stb = [sbt(es, tag + "stb%d" % i, [128, 512], BF16) for i in range(3)]
                stf = [sbt(es, tag + "stf%d" % i, [128, 512], F32) for i in range(3)]
                vst = [sbt(es, tag + "vst%d" % i, [128, 1024], BF16) for i in range(2)]
                xTb = [Buf() for _ in range(NQ)]
                wrowb = Buf()
                VNb = [Buf() for _ in range(16)]
                wchb = [Buf() for _ in range(3)]
                cst = [Buf() for _ in range(5)]
                vtb = [Buf() for _ in range(2)]
                vsqb, smb = Buf(), Buf()
                utb = [Buf() for _ in range(2)]
                stmpb = [Buf() for _ in range(2)]
                stbb = [Buf() for _ in range(3)]
                stfb = [Buf() for _ in range(3)]
                vstb = [Buf() for _ in range(2)]

                def load_xq(tok0, q):
                    P.dma("pool", xT[:, :, q * QW:(q + 1) * QW],
                          X1[:, tok0 + q * QW: tok0 + (q + 1) * QW].rearrange("(f p) t -> p f t", p=128),
                          xTb[q], writes=[xTb[q]])

                def load_xT(tok0):
                    for q in range(NQ):
                        load_xq(tok0, q)

                def load_wrow_part(which, k4):
                    P.dma("pool", wrow[:, k4 * 4:(k4 + 1) * 4, :],
                          WROW[which][:, k4 * 4096:(k4 + 1) * 4096].rearrange("p (k c) -> p k c", k=4),
                          wrowb, writes=[wrowb])

                def load_wrow(which):
                    for k4 in range(4):
                        load_wrow_part(which, k4)

                load_xq(0, 0)
                extra = [lambda q=q: load_xq(0, q) for q in range(1, NQ)] + [lambda k4=k4: load_wrow_part(0, k4) for k4 in range(4)]
                P.dma("sp", gbc[:], SGUG[0:1, :].partition_broadcast(128), cst[0], writes=[cst[0]])
                P.dma("sp", bbc[:], SGUB[0:1, :].partition_broadcast(128), cst[1], writes=[cst[1]])
                P.dma("sp", wst[:], SGUWT.rearrange("s (g t) -> s g t", g=8), cst[2], writes=[cst[2]])
                P.dma("sp", msk[:], SGUMASK, cst[3], writes=[cst[3]])
                P.dma("sp", sbias[:], SGUBIAS[0:1, :].partition_broadcast(128), cst[4], writes=[cst[4]])
                for g in range(8):
                    P.op("dve", lambda: nc.vector.tensor_tensor(out=wsb[:, g, :], in0=wst[:, g, :], in1=msk[:], op=ALU.mult),
                         reads=[cst[2], cst[3]], writes=[cst[2]] if g == 7 else [])
                wsbb = cst[2]

                def rowproj(blk, evac):
                    for half in range(2):
                        bk = bank()
                        P.mm(bankb[bk], banks[bk][:],
                             [(xT[:, kc, blk * 128:(blk + 1) * 128], wrow[:, kc, half * 512:(half + 1) * 512]) for kc in range(16)],
                             reads=[xTb[blk // 4], wrowb])
                        evac(half, bk)

                nch = [0]
                nst = [0]

                def chunk(c, per_q):
                    sl = nch[0] % 3
                    nch[0] += 1
                    P.dma("pool", wch[sl][:], WIN[c].rearrange("p (k c) -> p k c", k=16), wchb[sl], writes=[wchb[sl]])
                    if extra:
                        nx = 3 if len(extra) == 7 else 1
                        for _ in range(nx):
                            extra.pop(0)()
                    for q in range(NQ):
                        qs = slice(q * QW, (q + 1) * QW)
                        bk = bank()
                        P.mm(bankb[bk], banks[bk][:], [(wch[sl][:, kc, :], xT[:, kc, qs]) for kc in range(16)],
                             reads=[wchb[sl], xTb[q]])
                        per_q(q, qs, bk)

                def qk_chunk(c, dst, row0, tok0, scale):
                    def per_q(q, qs, bk):
                        k3 = nst[0] % 3
                        nst[0] += 1
                        P.op("act", lambda: nc.scalar.mul(out=stb[k3][:], in_=banks[bk][:], mul=scale),
                             reads=[bankb[bk]], writes=[stbb[k3]])
                        P.dma("act", dst[row0:row0 + 128, tok0 + q * QW: tok0 + (q + 1) * QW], stb[k3][:], stbb[k3], reads=[stbb[k3]])
                    chunk(c, per_q)

                for c in range(8):
                    qk_chunk(16 + c, QT, c * 128, 0, 0.125)
                for c in range(8):
                    qk_chunk(24 + c, KT, c * 128, 0, 1.0)
                for blk in range(16):
                    v_t = vt[blk % 2]
                    vb = vtb[blk % 2]

                    def ev(half, bk):
                        P.op("act", lambda: nc.scalar.activation(out=v_t[:, half * 512:(half + 1) * 512], in_=banks[bk][:],
                                                                 func=AF.Gelu_apprx_tanh),
                             reads=[bankb[bk]], writes=[vb])
                    rowproj(blk, ev)
                    P.op("dve", lambda: nc.vector.reduce_sum(out=sm[:, 0:1], in_=v_t[:], axis=mybir.AxisListType.X), reads=[vb], writes=[smb])
                    P.op("dve", lambda: nc.vector.tensor_tensor(out=vsq[:], in0=v_t[:], in1=v_t[:], op=ALU.mult), reads=[vb], writes=[vsqb])
                    P.op("dve", lambda: nc.vector.reduce_sum(out=sm[:, 1:2], in_=vsq[:], axis=mybir.AxisListType.X), reads=[vsqb], writes=[smb])
                    P.op("dve", lambda: nc.vector.tensor_scalar(out=sm[:, 2:4], in0=sm[:, 0:2], scalar1=1.0 / 1024, scalar2=None, op0=ALU.mult),
                         reads=[smb], writes=[smb])
                    P.op("dve", lambda: nc.vector.tensor_tensor(out=sm[:, 4:5], in0=sm[:, 2:3], in1=sm[:, 2:3], op=ALU.mult), reads=[smb], writes=[smb])
                    P.op("dve", lambda: nc.vector.tensor_tensor(out=sm[:, 5:6], in0=sm[:, 3:4], in1=sm[:, 4:5], op=ALU.subtract), reads=[smb], writes=[smb])
                    P.op("act", lambda: nc.scalar.activation(out=sm[:, 6:7], in_=sm[:, 5:6], func=AF.Sqrt, bias=epsc[:, 0:1], scale=1.0),
                         reads=[smb] + constb, writes=[smb])
                    P.op("dve", lambda: nc.vector.reciprocal(out=sm[:, 7:8], in_=sm[:, 6:7]), reads=[smb], writes=[smb])
                    P.op("dve", lambda: nc.vector.tensor_scalar(out=v_t[:], in0=v_t[:], scalar1=sm[:, 2:3], scalar2=sm[:, 7:8],
                                                                op0=ALU.subtract, op1=ALU.mult), reads=[vb, smb], writes=[vb])
                    P.op("dve", lambda: nc.vector.tensor_tensor(out=v_t[:], in0=v_t[:], in1=gbc[:], op=ALU.mult), reads=[vb, cst[0]], writes=[vb])
                    P.op("dve", lambda: nc.vector.tensor_tensor(out=VN[:, blk, :], in0=v_t[:], in1=bbc[:], op=ALU.add),
                         reads=[vb, cst[1]], writes=[VNb[blk]])

                for g in range(8):
                    def per_q(q, qs, bk, g=g):
                        k2 = nst[0] % 2
                        nst[0] += 1
                        P.op("act", lambda: nc.scalar.activation(out=ut[k2][:], in_=banks[bk][:], func=AF.Gelu_apprx_tanh),
                             reads=[bankb[bk]], writes=[utb[k2]])
                        bs = bank()

                        def fn():
                            ins = None
                            for k in range(4):
                                ins = nc.tensor.matmul(banks[bs][:, k * 128:(k + 1) * 128], lhsT=VN[:, 4 * q + k, g * 128:(g + 1) * 128],
                                                       rhs=wsb[:, g, :], start=True, stop=True)
                            return ins
                        P.op("pe", fn, reads=[VNb[4 * q + k] for k in range(4)] + [wsbb], writes=[bankb[bs]])
                        for k in range(4):
                            ks = slice(k * 128, (k + 1) * 128)
                            P.op("dve", lambda: nc.vector.tensor_tensor(out=stmp[k2][:, ks], in0=banks[bs][:, ks], in1=sbias[:, g * 128:(g + 1) * 128], op=ALU.add),
                                 reads=[bankb[bs], cst[4]], writes=[stmpb[k2]])
                        k3 = (nst[0] - 1) % 3
                        P.op("dve", lambda: nc.vector.tensor_tensor(out=stb[k3][:], in0=stmp[k2][:], in1=ut[k2][:], op=ALU.mult),
                             reads=[stmpb[k2], utb[k2]], writes=[stbb[k3]])
                        P.dma("sp", YA[g * 128:(g + 1) * 128, qs], stb[k3][:], stbb[k3], reads=[stbb[k3]])
                    chunk(g, per_q)

                for c in range(32):
                    dst = GA if c < 16 else GB
                    row0 = (c % 16) * 128

                    def per_q(q, qs, bk, dst=dst, row0=row0):
                        k3 = nst[0] % 3
                        nst[0] += 1
                        P.op("act", lambda: nc.scalar.activation(out=stf[k3][:], in_=banks[bk][:], func=AF.Sigmoid),
                             reads=[bankb[bk]], writes=[stfb[k3]])
                        P.dma("act", dst[row0:row0 + 128, qs], stf[k3][:], stfb[k3], reads=[stfb[k3]])
                    chunk(40 + c, per_q)

                def val_blocks(tok0):
                    for blk in range(16):
                        vs = vst[blk % 2]
                        vsb_ = vstb[blk % 2]

                        def ev(half, bk):
                            P.op("dve", lambda: nc.vector.tensor_copy(out=vs[:, half * 512:(half + 1) * 512], in_=banks[bk][:]),
                                 reads=[bankb[bk]], writes=[vsb_])
                        rowproj(blk, ev)
                        P.dma("sp", VAL[tok0 + blk * 128: tok0 + (blk + 1) * 128, :], vs[:], vsb_, reads=[vsb_])

                load_wrow(1)
                val_blocks(0)
                load_xT(OWN)
                for c in range(8):
                    qk_chunk(24 + c, KT, c * 128, OWN, 1.0)
                val_blocks(OWN)
                P.barrier()

        def attn_phase():
            with ExitStack() as es:
                tag = "at"
                KA = [[sbt(es, tag + "ka%d%d" % (s, i), [128, ALL], BF16) for i in range(2)] for s in range(2)]
                QA = [[sbt(es, tag + "qa%d%d" % (s, i), [128, OWN], BF16) for i in range(2)] for s in range(2)]
                VH = [sbt(es, tag + "vh%d" % i, [128, 32, 128], BF16) for i in range(2)]
                DG = [sbt(es, tag + "dg%d" % i, [128, 128], BF16) for i in range(2)]
                ident = sbt(es, tag + "ident", [128, 128], BF16)
                pt = [sbt(es, tag + "pt%d" % i, [128, 512], BF16) for i in range(4)]
                ctmp = [sbt(es, tag + "ct%d" % i, [128, 128], F32) for i in range(2)]
                o1 = [sbt(es, tag + "o1%d" % i, [128, 512], F32) for i in range(2)]
                o2 = [sbt(es, tag + "o2%d" % i, [128, 512], F32) for i in range(2)]
                d1 = [sbt(es, tag + "d1%d" % i, [128, 512], F32) for i in range(2)]
                d2 = [sbt(es, tag + "d2%d" % i, [128, 512], F32) for i in range(2)]
                sqo = [sbt(es, tag + "sqo%d" % i, [128, 512], F32) for i in range(2)]
                rst = [sbt(es, tag + "rst%d" % i, [128, 512], F32) for i in range(2)]
                ybs = [sbt(es, tag + "ybs%d" % i, [128, 512], BF16) for i in range(2)]
                lamt = sbt(es, tag + "lamt", [128, 4, 64], F32)
                lamp = sbt(es, tag + "lamp", [128, 64], F32)
                lsm = sbt(es, tag + "lsm", [128, 8], F32)
                ang = sbt(es, tag + "ang", [128, 8], F32)
                pmask = sbt(es, tag + "pmask", [128, 128], BF16)
                eps128 = sbt(es, tag + "eps128", [128, 1], F32)
                KAb = [[Buf() for _ in range(2)] for _ in range(2)]
                QAb = [[Buf() for _ in range(2)] for _ in range(2)]
                VHb = [Buf() for _ in range(2)]
                DGb = [Buf() for _ in range(2)]
                ptb = [Buf() for _ in range(4)]
                ctb = [Buf() for _ in range(2)]
                o1b = [Buf() for _ in range(2)]
                o2b = [Buf() for _ in range(2)]
                d1b = [Buf() for _ in range(2)]
                d2b = [Buf() for _ in range(2)]
                sqob = [Buf() for _ in range(2)]
                rstb = [Buf() for _ in range(2)]
                ybsb = [Buf() for _ in range(2)]
                lamb, angb, pmb = Buf(), Buf(), Buf()
                lampb, lsmb = Buf(), Buf()
                for k in range(4):
                    P.dma("sp", lamt[:, k, :], LAMV[k:k + 1, :].partition_broadcast(128), lamb, writes=[lamb])
                P.dma("sp", ang[:], ANG, angb, writes=[angb])
                P.dma("sp", pmask[:], PMASK, pmb, writes=[pmb])
                identb = Buf()
                P.dma("sp", ident[:], IDENT, identb, writes=[identb])
                epsb = Buf()
                P.op("dve", lambda: nc.vector.memset(eps128[:], LN_EPS), writes=[epsb])
                for k in range(2):
                    P.op("dve", lambda: nc.vector.tensor_tensor(out=lamp[:], in0=lamt[:, 2 * k, :], in1=lamt[:, 2 * k + 1, :], op=ALU.mult),
                         reads=[lamb], writes=[lampb])
                    P.op("dve", lambda: nc.vector.reduce_sum(out=lsm[:, k:k + 1], in_=lamp[:], axis=mybir.AxisListType.X), reads=[lampb], writes=[lsmb])
                P.op("act", lambda: nc.scalar.activation(out=lsm[:, 2:4], in_=lsm[:, 0:2], func=AF.Exp), reads=[lsmb], writes=[lsmb])
                P.op("dve", lambda: nc.vector.tensor_tensor(out=lsm[:, 4:5], in0=lsm[:, 2:3], in1=lsm[:, 3:4], op=ALU.subtract), reads=[lsmb], writes=[lsmb])
                P.op("dve", lambda: nc.vector.tensor_scalar(out=lsm[:, 5:6], in0=lsm[:, 4:5], scalar1=LAM_INIT, scalar2=-1.0, op0=ALU.add, op1=ALU.mult),
                     reads=[lsmb], writes=[lsmb])
                P.op("dve", lambda: nc.vector.tensor_scalar(out=ang[:], in0=ang[:], scalar1=(1.0 - LAM_INIT), scalar2=None, op0=ALU.mult),
                     reads=[angb], writes=[angb])
                SB = [0, 1, 2, 5]
                NSB = 4
                O = [3, 4]
                Dn = [6, 7]
                MS = 6
                dacc = [[sbt(es, tag + "dacc%d%d" % (s_, t_), [128, 512], F32) for t_ in range(2)] for s_ in range(2)]
                daccb = [[Buf() for _ in range(2)] for _ in range(2)]

                def load_head(h):
                    hb = h % 2
                    for s in range(2):
                        r0 = h * 128 + s * 64
                        P.dma("sp", KA[s][hb][0:64, :], KT[r0:r0 + 64, :], KAb[s][hb], writes=[KAb[s][hb]])
                        P.dma("sp", KA[s][hb][64:68, :], POSK[h], KAb[s][hb], writes=[KAb[s][hb]])
                        P.dma("sp", QA[s][hb][0:64, :], QT[r0:r0 + 64, :], QAb[s][hb], writes=[QAb[s][hb]])
                        P.dma("sp", QA[s][hb][64:68, :], POSQ[h], QAb[s][hb], writes=[QAb[s][hb]])
                    P.dma("sp", VH[hb][:], VAL[:, h * 128:(h + 1) * 128].rearrange("(n p) e -> p n e", p=128), VHb[hb], writes=[VHb[hb]])
                    P.dma("sp", DG[hb][:], DIAG[h], DGb[hb], writes=[DGb[hb]])

                units = []
                for h in range(8):
                    for i in range(NQ):
                        blocks = []
                        for jb in range(4 * i):
                            blocks.append((jb, 0, "full"))
                        for jb in range(4 * i):
                            blocks.append((16 + jb, 0, "full"))
                        for m in range(4):
                            blocks.append((4 * i + m, m * 128, "own"))
                        for m in range(4):
                            blocks.append((16 + 4 * i + m, m * 128, "oth"))
                        assert blocks[0][1] == 0
                        nblk = len(blocks)
                        for bi, (kb, c0, kind) in enumerate(blocks):
                            for s in range(2):
                                units.append((h, i, bi, nblk, kb, c0, kind, s))
                NU = len(units)
                LOOK = 3

                def emit_S(k):
                    h, i, bi, nblk, kb, c0, kind, s = units[k]
                    hb = h % 2
                    sb_ = SB[k % NSB]
                    q0 = i * QW
                    if kind == "full":
                        P.mm(bankb[sb_], banks[sb_][:, c0:512],
                             [(KA[s][hb][0:68, kb * 128:(kb + 1) * 128], QA[s][hb][0:68, q0 + c0:q0 + 512])],
                             reads=[KAb[s][hb], QAb[s][hb]])
                    else:
                        fix = DG[hb] if kind == "own" else pmask
                        fixb = DGb[hb] if kind == "own" else pmb

                        def fn():
                            nc.tensor.matmul(banks[sb_][:, c0:512], lhsT=KA[s][hb][0:68, kb * 128:(kb + 1) * 128],
                                             rhs=QA[s][hb][0:68, q0 + c0:q0 + 512], start=True, stop=False)
                            return nc.tensor.matmul(banks[sb_][:, c0:c0 + 128], lhsT=ident[:], rhs=fix[:], start=False, stop=True)
                        P.op("pe", fn, reads=[KAb[s][hb], QAb[s][hb], fixb, identb], writes=[bankb[sb_]])

                pending = []

                def fin_a(h, i):
                    t = (h * NQ + i) % 2
                    P.op("act", lambda: nc.scalar.copy(out=o1[t][:], in_=banks[O[0]][:]), reads=[bankb[O[0]]], writes=[o1b[t]])
                    P.op("act", lambda: nc.scalar.copy(out=o2[t][:], in_=banks[O[1]][:]), reads=[bankb[O[1]]], writes=[o2b[t]])
                    P.mm(bankb[Dn[0]], banks[Dn[0]][:], [(ones32[:], dacc[0][t][:])], reads=[daccb[0][t]] + constb)
                    P.mm(bankb[Dn[1]], banks[Dn[1]][:], [(ones32[:], dacc[1][t][:])], reads=[daccb[1][t]] + constb)
                    P.op("act", lambda: nc.scalar.activation(out=d1[t][:], in_=banks[Dn[0]][:], func=AF.Ln), reads=[bankb[Dn[0]]], writes=[d1b[t]])
                    P.op("act", lambda: nc.scalar.activation(out=d1[t][:], in_=d1[t][:], func=AF.Exp, scale=-1.0), reads=[d1b[t]], writes=[d1b[t]])
                    P.op("act", lambda: nc.scalar.activation(out=d2[t][:], in_=banks[Dn[1]][:], func=AF.Ln), reads=[bankb[Dn[1]]], writes=[d2b[t]])
                    P.op("act", lambda: nc.scalar.activation(out=d2[t][:], in_=d2[t][:], func=AF.Exp, scale=-1.0), reads=[d2b[t]], writes=[d2b[t]])
                    P.op("dve", lambda: nc.vector.tensor_tensor(out=o1[t][:], in0=o1[t][:], in1=d1[t][:], op=ALU.mult),
                         reads=[o1b[t], d1b[t]], writes=[o1b[t]])
                    P.op("dve", lambda: nc.vector.tensor_tensor(out=o2[t][:], in0=o2[t][:], in1=d2[t][:], op=ALU.mult),
                         reads=[o2b[t], d2b[t]], writes=[o2b[t]])
                    P.op("dve", lambda: nc.vector.scalar_tensor_tensor(out=o1[t][:], in0=o2[t][:], scalar=lsm[:, 5:6], in1=o1[t][:],
                                                                       op0=ALU.mult, op1=ALU.add),
                         reads=[o2b[t], o1b[t], lsmb], writes=[o1b[t]])
                    P.op("dve", lambda: nc.vector.tensor_tensor(out=sqo[t][:], in0=o1[t][:], in1=o1[t][:], op=ALU.mult),
                         reads=[o1b[t]], writes=[sqob[t]])

                def fin_b(h, i):
                    t = (h * NQ + i) % 2
                    q0 = i * QW
                    P.mm(bankb[MS], banks[MS][:], [(ones32[:], sqo[t][:])], reads=[sqob[t]] + constb)
                    P.op("act", lambda: nc.scalar.activation(out=rst[t][:], in_=banks[MS][:], func=AF.Ln, bias=eps128[:, 0:1], scale=1.0 / 128),
                         reads=[bankb[MS], epsb], writes=[rstb[t]])
                    P.op("act", lambda: nc.scalar.activation(out=rst[t][:], in_=rst[t][:], func=AF.Exp, scale=-0.5), reads=[rstb[t]], writes=[rstb[t]])
                    P.op("pool", lambda: nc.gpsimd.tensor_tensor(out=o1[t][:], in0=o1[t][:], in1=rst[t][:], op=ALU.mult),
                         reads=[o1b[t], rstb[t]], writes=[o1b[t]])
                    P.op("dve", lambda: nc.vector.tensor_scalar(out=ybs[t][:], in0=o1[t][:], scalar1=ang[:, h:h + 1], scalar2=None, op0=ALU.mult),
                         reads=[o1b[t], angb], writes=[ybsb[t]])
                    P.dma("sp", YBD[h * 128:(h + 1) * 128, q0:q0 + QW], ybs[t][:], ybsb[t], reads=[ybsb[t]])

                load_head(0)
                for k in range(min(LOOK, NU)):
                    emit_S(k)
                for k in range(NU):
                    h, i, bi, nblk, kb, c0, kind, s = units[k]
                    hb = h % 2
                    if i == 0 and bi == 0 and s == 0 and h + 1 < 8:
                        load_head(h + 1)
                    if k + LOOK < NU:
                        emit_S(k + LOOK)
                    sb_ = SB[k % NSB]
                    pi = k % 4
                    p_t = pt[pi]
                    P.op("act", lambda: nc.scalar.activation(out=p_t[:, c0:512], in_=banks[sb_][:, c0:512], func=AF.Exp),
                         reads=[bankb[sb_]], writes=[ptb[pi]])
                    P.mm(bankb[O[s]], banks[O[s]][:, c0:512], [(VH[hb][:, kb, :], p_t[:, c0:512])],
                         reads=[VHb[hb], ptb[pi]], start=(bi == 0), stop=(bi == nblk - 1))
                    tt = (h * NQ + i) % 2
                    if bi == 0:
                        P.op("dve", lambda: nc.vector.tensor_copy(out=dacc[s][tt][:], in_=p_t[:]), reads=[ptb[pi]], writes=[daccb[s][tt]])
                    else:
                        P.op("dve", lambda: nc.vector.tensor_tensor(out=dacc[s][tt][:, c0:512], in0=dacc[s][tt][:, c0:512], in1=p_t[:, c0:512], op=ALU.add),
                             reads=[ptb[pi], daccb[s][tt]], writes=[daccb[s][tt]])
                    for ent in pending:
                        ent[0] -= 1
                    while pending and pending[0][0] <= 0:
                        ent = pending.pop(0)
                        fin_b(ent[1], ent[2])
                    if bi == nblk - 1 and s == 1:
                        fin_a(h, i)
                        pending.append([12, h, i])
                while pending:
                    ent = pending.pop(0)
                    fin_b(ent[1], ent[2])
                P.barrier()

        def merge_phase():
            with ExitStack() as es0:
                MG = sbt(es0, "mgMG", [128, 16, OWN], BF16)
                MGb = [[Buf() for _ in range(NQ)] for _ in range(16)]
                with ExitStack() as es:
                    tag = "m1"
                    YAs = sbt(es, tag + "ya", [128, 8, OWN], BF16)
                    YBs = sbt(es, tag + "yb", [128, 8, OWN], BF16)
                    wab = [sbt(es, tag + "wab%d" % i, [128, 2, 8, 128], BF16) for i in range(2)]
                    gat = [sbt(es, tag + "ga%d" % i, [128, 512], F32) for i in range(4)]
                    gbt = [sbt(es, tag + "gb%d" % i, [128, 512], F32) for i in range(4)]
                    t1 = [sbt(es, tag + "t1%d" % i, [128, 512], F32) for i in range(2)]
                    yab, ybb = Buf(), Buf()
                    wabb = [Buf() for _ in range(2)]
                    gatb = [Buf() for _ in range(4)]
                    gbtb = [Buf() for _ in range(4)]
                    t1b = [Buf() for _ in range(2)]
                    P.dma("sp", YAs[:], YA.rearrange("(k p) t -> p k t", p=128), yab, writes=[yab])
                    P.dma("sp", YBs[:], YBD.rearrange("(k p) t -> p k t", p=128), ybb, writes=[ybb])
                    items = [(f, q) for f in range(16) for q in range(NQ)]

                    def load(i):
                        f, q = items[i]
                        qs = slice(q * QW, (q + 1) * QW)
                        P.dma("sp", gat[i % 4][:], GA[f * 128:(f + 1) * 128, qs], gatb[i % 4], writes=[gatb[i % 4]])
                        P.dma("sp", gbt[i % 4][:], GB[f * 128:(f + 1) * 128, qs], gbtb[i % 4], writes=[gbtb[i % 4]])
                    for i in range(3):
                        load(i)
                    for i, (f, q) in enumerate(items):
                        qs = slice(q * QW, (q + 1) * QW)
                        if q == 0:
                            sl = f % 2
                            P.dma("pool", wab[sl][:, 0, :, :], WA[f].rearrange("p (k c) -> p k c", k=8), wabb[sl], writes=[wabb[sl]])
                            P.dma("pool", wab[sl][:, 1, :, :], WB[f].rearrange("p (k c) -> p k c", k=8), wabb[sl], writes=[wabb[sl]])
                        if i + 3 < len(items):
                            load(i + 3)
                        ba = bank()
                        P.mm(bankb[ba], banks[ba][:], [(wab[sl][:, 0, kc, :], YAs[:, kc, qs]) for kc in range(8)], reads=[wabb[sl], yab])
                        bb = bank()
                        P.mm(bankb[bb], banks[bb][:], [(wab[sl][:, 1, kc, :], YBs[:, kc, qs]) for kc in range(8)], reads=[wabb[sl], ybb])
                        P.op("dve", lambda: nc.vector.tensor_tensor(out=gat[i % 4][:], in0=banks[ba][:], in1=gat[i % 4][:], op=ALU.mult),
                             reads=[bankb[ba], gatb[i % 4]], writes=[gatb[i % 4]])
                        P.op("dve", lambda: nc.vector.tensor_tensor(out=gbt[i % 4][:], in0=banks[bb][:], in1=gbt[i % 4][:], op=ALU.mult),
                             reads=[bankb[bb], gbtb[i % 4]], writes=[gbtb[i % 4]])
                        P.op("dve", lambda: nc.vector.tensor_tensor(out=MG[:, f, qs], in0=gat[i % 4][:], in1=gbt[i % 4][:], op=ALU.add),
                             reads=[gatb[i % 4], gbtb[i % 4]], writes=[MGb[f][q]])
                P.barrier()
                with ExitStack() as es:
                    tag = "m2"
                    wo = sbt(es, tag + "wo", [128, 16, 16, 128], BF16)
                    wob = [Buf() for _ in range(16)]
                    sin = [sbt(es, tag + "sin%d" % i, [128, 512], F32) for i in range(4)]
                    sinb = [Buf() for _ in range(4)]
                    lout = [sbt(es, tag + "lout%d" % i, [128, 512], F32) for i in range(3)]
                    loutb = [Buf() for _ in range(3)]
                    for f in range(16):
                        P.dma("pool", wo[:, f, :, :], WO[f].rearrange("p (k c) -> p k c", k=16), wob[f], writes=[wob[f]])

                    def prefetch(i):
                        q, f = divmod(i, 16)
                        P.dma("sp", sin[i % 4][:], X1[f * 128:(f + 1) * 128, q * QW:(q + 1) * QW], sinb[i % 4], writes=[sinb[i % 4]])

                    def produce(q, f, i, ydst, ydb):
                        qs = slice(q * QW, (q + 1) * QW)
                        bk = bank()
                        P.mm(bankb[bk], banks[bk][:], [(wo[:, f, kc, :], MG[:, kc, qs]) for kc in range(16)],
                             reads=[wob[f]] + [MGb[kc][q] for kc in range(16)])
                        si = sin[i % 4]
                        P.op("act", lambda: nc.scalar.mul(out=si[:], in_=si[:], mul=ALPHA), reads=[sinb[i % 4]], writes=[sinb[i % 4]])
                        P.op("dve", lambda: nc.vector.tensor_tensor(out=ydst, in0=banks[bk][:], in1=si[:], op=ALU.add),
                             reads=[bankb[bk], sinb[i % 4]], writes=[ydb])

                    ln_pipeline(es, tag, X2, 0, 1, produce, prefetch, 3, lout, loutb)
                P.barrier()

        def pe_phase():
            with ExitStack() as es:
                tag = "pe"
                xq = [sbt(es, tag + "xq%d" % i, [128, 16, QW], BF16) for i in range(2)]
                pT = sbt(es, tag + "pT", [128, 2, OWN], BF16)
                wg = sbt(es, tag + "wg", [128, 16, 16, 128], BF16)
                wp = sbt(es, tag + "wp", [128, 16, 2, 128], BF16)
                sgt = [sbt(es, tag + "sg%d" % i, [128, 512], F32) for i in range(2)]
                sin = [sbt(es, tag + "sin%d" % i, [128, 512], F32) for i in range(4)]
                lout = [sbt(es, tag + "lout%d" % i, [128, 512], F32) for i in range(3)]
                xqb = [Buf() for _ in range(2)]
                pTb, wpb = Buf(), Buf()
                wgb = [Buf() for _ in range(16)]
                sgb = [Buf() for _ in range(2)]
                sinb = [Buf() for _ in range(4)]
                loutb = [Buf() for _ in range(3)]

                def load_xq(q):
                    P.dma("pool", xq[q % 2][:], X3[:, q * QW:(q + 1) * QW].rearrange("(f p) t -> p f t", p=128),
                          xqb[q % 2], writes=[xqb[q % 2]])

                load_xq(0)
                P.dma("pool", pT[:], PT.rearrange("(k p) t -> p k t", p=128), pTb, writes=[pTb])
                P.dma("pool", wp[:], WP.rearrange("f p (k c) -> p f k c", k=2), wpb, writes=[wpb])
                for f in range(16):
                    P.dma("pool", wg[:, f, :, :], WG[f].rearrange("p (k c) -> p k c", k=16), wgb[f], writes=[wgb[f]])
                    if f == 3:
                        load_xq(1)

                def prefetch(i):
                    q, f = divmod(i, 16)
                    P.dma("sp", sin[i % 4][:], X3[f * 128:(f + 1) * 128, q * QW:(q + 1) * QW], sinb[i % 4], writes=[sinb[i % 4]])

                def produce(q, f, i, ydst, ydb):
                    qs = slice(q * QW, (q + 1) * QW)
                    if f == 0 and q >= 1 and q + 1 < NQ:
                        load_xq(q + 1)
                    xt = xq[q % 2]
                    bg = bank()
                    P.mm(bankb[bg], banks[bg][:], [(wg[:, f, kc, :], xt[:, kc, :]) for kc in range(16)], reads=[wgb[f], xqb[q % 2]])
                    bp = bank()
                    P.mm(bankb[bp], banks[bp][:], [(wp[:, f, kc, :], pT[:, kc, qs]) for kc in range(2)], reads=[wpb, pTb])
                    k2 = i % 2
                    si = sin[i % 4]
                    P.op("act", lambda: nc.scalar.activation(out=sgt[k2][:], in_=banks[bg][:], func=AF.Sigmoid), reads=[bankb[bg]], writes=[sgb[k2]])
                    P.op("dve", lambda: nc.vector.tensor_tensor(out=sgt[k2][:], in0=banks[bp][:], in1=sgt[k2][:], op=ALU.mult),
                         reads=[bankb[bp], sgb[k2]], writes=[sgb[k2]])
                    P.op("dve", lambda: nc.vector.scalar_tensor_tensor(out=ydst, in0=si[:], scalar=ALPHA, in1=sgt[k2][:],
                                                                       op0=ALU.mult, op1=ALU.add),
                         reads=[sgb[k2], sinb[i % 4]], writes=[ydb])

                ln_pipeline(es, tag, OUT, 0, 3, produce, prefetch, 3, lout, loutb)
                P.barrier()

        ffn_phase(X0, X1, 0, WGU[0], WDN[0], 0, "f1a")
        if STAGE >= 2:
            ffn_phase(X0, X1, OWN, WGU[0], WDN[0], 0, "f1b")
        if STAGE >= 3:
            proj_phase()
        if STAGE >= 4:
            attn_phase()
        if STAGE >= 5:
            merge_phase()
        if STAGE >= 6:
            ffn_phase(X2, X3, 0, WGU[1], WDN[1], 2, "f2")
        if STAGE >= 7:
            pe_phase()

        P.barrier()
    return nc


def _prep_shared(inp):
    sh = {}
    ffw = {1: (inp["ffn1_w_gu"], inp["ffn1_w_down"]), 2: (inp["ffn2_w_gu"], inp["ffn2_w_down"])}
    for i in (1, 2):
        wgu = np.asarray(ffw[i][0][0], np.float32)
        t = wgu.reshape(16, 128, 2, NJ, 128)
        sh["wgu%d" % i] = np.ascontiguousarray(t.transpose(3, 1, 2, 0, 4)).reshape(NJ, 128, 2 * 16 * 128)
        wd = np.asarray(ffw[i][1][0], np.float32)
        t = wd.reshape(NPART, JP, 128, 8, 2, 128)
        sh["wdn%d" % i] = np.ascontiguousarray(t.transpose(0, 3, 2, 4, 1, 5)).reshape(NPART, 8, 128, 2 * JP * 128)
    win = np.asarray(inp["w_in"][0], np.float32)
    t = win.reshape(16, 128, 72, 128)
    sh["win"] = np.ascontiguousarray(t.transpose(2, 1, 0, 3)).reshape(72, 128, 16 * 128)
    rows = []
    for c0 in (1024, 4096):
        blk = win[:, c0:c0 + 1024].reshape(16, 128, 1024)
        rows.append(np.ascontiguousarray(blk.transpose(1, 0, 2)).reshape(128, 16 * 1024))
    sh["wrow"] = np.stack(rows)

    def ftile(w, nk):
        t = np.asarray(w, np.float32).reshape(nk, 128, 16, 128)
        return np.ascontiguousarray(t.transpose(2, 1, 0, 3)).reshape(16, 128, nk * 128)

    sh["wa"] = ftile(inp["w_branch_a"][0], 8)
    sh["wb"] = ftile(inp["w_branch_b"][0], 8)
    sh["wo"] = ftile(inp["w_out"][0], 16)
    sh["wg"] = ftile(inp["w_pe_gate"][0], 16)
    sh["wp"] = ftile(inp["w_pe_proj"][0], 2)
    lgs = [inp["ln1_g"], inp["ln2_g"], inp["ln3_g"], inp["ln4_g"]]
    lbs = [inp["ln1_b"], inp["ln2_b"], inp["ln3_b"], inp["ln4_b"]]
    sh["lng"] = np.stack([np.ascontiguousarray(np.asarray(a[0], np.float32).reshape(16, 128).T) for a in lgs])
    sh["lnb"] = np.stack([np.ascontiguousarray(np.asarray(a[0], np.float32).reshape(16, 128).T) for a in lbs])
    sh["sgug"] = np.asarray(inp["sgu_ln_g"], np.float32).reshape(1, 1024)
    sh["sgub"] = np.asarray(inp["sgu_ln_b"], np.float32).reshape(1, 1024)
    sw = np.asarray(inp["sgu_w"][0], np.float32)
    sh["sguwt"] = np.ascontiguousarray(sw.transpose(2, 0, 1)).reshape(128, 8 * 128)
    pos = np.arange(128)
    sh["sgumask"] = ((pos[:, None] // 64) <= (pos[None, :] // 64)).astype(np.float32)
    sh["sgubias"] = np.asarray(inp["sgu_b"][0], np.float32).reshape(1, 8 * 128)
    sh["lamv"] = np.stack([np.asarray(a[0], np.float32) for a in (inp["lam_q1"], inp["lam_k1"], inp["lam_q2"], inp["lam_k2"])])
    sh["ang"] = np.ascontiguousarray(np.asarray(inp["attn_norm_g"][0], np.float32).reshape(8, 128).T)
    slopes = 2.0 ** (-np.arange(1, 9, dtype=np.float64))
    kl = pos[:, None]
    ql = pos[None, :]
    allowed = (kl // 64) <= (ql // 64)
    diag = np.zeros((8, 128, 128), np.float32)
    for hh in range(8):
        corr = -2.0 * slopes[hh] * np.maximum(kl - ql, 0)
        diag[hh] = np.where(allowed, corr, NEG)
    sh["diag"] = diag.astype(ml_dtypes.bfloat16)
    sh["ident"] = np.eye(128, dtype=np.float32).astype(ml_dtypes.bfloat16)
    return sh, slopes


def _prep_core(inp, sh, slopes, b, r):
    m = dict(sh)
    x = np.asarray(inp["x"][b], np.float32).reshape(32, 128, D)
    own = x[r::2].reshape(OWN, D)
    oth = x[(1 - r)::2].reshape(OWN, D)
    m["x0"] = np.ascontiguousarray(np.concatenate([own, oth], 0).T)
    p = np.asarray(inp["p"][0, b], np.float32).reshape(32, 128, 256)[r::2].reshape(OWN, 256)
    m["pt"] = np.ascontiguousarray(p.T)
    blk = np.arange(16)
    gpos_own = ((2 * blk + r)[:, None] * 128 + np.arange(128)[None, :]).reshape(-1)
    gpos_oth = ((2 * blk + (1 - r))[:, None] * 128 + np.arange(128)[None, :]).reshape(-1)
    gk = np.concatenate([gpos_own, gpos_oth])
    posk = np.zeros((8, 4, ALL), np.float32)
    posq = np.zeros((8, 4, OWN), np.float32)
    for hh in range(8):
        s = slopes[hh]
        posk[hh, 0] = 1.0
        posk[hh, 1] = 1.0
        posk[hh, 2] = s * 128.0 * (gk // 128)
        posk[hh, 3] = s * (gk % 128)
        posq[hh, 0] = -s * 128.0 * (gpos_own // 128)
        posq[hh, 1] = -s * (gpos_own % 128)
        posq[hh, 2] = 1.0
        posq[hh, 3] = 1.0
    m["posk"] = posk.astype(ml_dtypes.bfloat16)
    m["posq"] = posq.astype(ml_dtypes.bfloat16)
    m["pmask"] = np.full((128, 128), NEG if r == 0 else 0.0, np.float32).astype(ml_dtypes.bfloat16)
    return m


_NC_CACHE = {}


def kernel(**inputs):
    sh, slopes = _prep_shared(inputs)
    in_maps = []
    for c in range(8):
        in_maps.append(_prep_core(inputs, sh, slopes, c // 2, c % 2))
    if "nc" not in _NC_CACHE:
        _NC_CACHE["nc"] = build_program()
    nc = _NC_CACHE["nc"]
    res = run_bass_kernel_spmd(nc, in_maps, core_ids=list(range(8)))
    if STAGE < 99:
        return res
    out = np.zeros((NBATCH, 32, 128, D), np.float32)
    for c in range(8):
        b, r = c // 2, c % 2
        o = np.asarray(res.results[c]["out"], np.float32)
        out[b, r::2] = o.T.reshape(16, 128, D)
    return out.reshape(NBATCH, SEQ, D)
```
